# Optimizing a Trainium2 kernel written in Bass

```python
import jax, jax.numpy as jnp
from jax import lax
import numpy as np

D_MODEL = 2048
BATCH = 32
SEQ = 256
DEPTH = 4
DEC_BATCH = 4
DEC_SEQ = 2048
PAST_LEN = 512

GRID_W = 64
N_MIXERS = 3
N_CONV_LAYERS = (DEPTH + 2) // 3
N_GMLP_LAYERS = (DEPTH + 1) // 3
N_MLA_LAYERS = DEPTH // 3
CONV_WIDTH = 3
CHUNK = 128
GMLP_WIDTH = D_MODEL
GMLP_GROUPS = 16
GMLP_GROUP_DIM = GMLP_WIDTH // GMLP_GROUPS
N_HEADS = 16
QK_NOPE = 128
QK_ROPE = 64
V_DIM = 128
Q_RANK = 512
KV_RANK = 512
ROPE_THETA = 10000.0
D_FF = 4 * D_MODEL
N_MOD = 6
EPS = 1e-6
Q_BLOCK = 128

kernel_name = 'hybrid_dit_conv_gmlp_mla_step'


def rms_norm(x, g):
    x32 = x.astype(jnp.float32)
    y = x32 * lax.rsqrt(jnp.mean(x32 * x32, axis=-1, keepdims=True) + EPS)
    return (y * g.astype(jnp.float32)).astype(x.dtype)


def modulate(h, shift, scale):
    return h * (1 + scale) + shift


def short_conv_mixer(h, w_in, w_conv, w_out):
    L = h.shape[1]
    bg, cg, hv = jnp.split(h @ w_in, 3, axis=-1)
    pad = CONV_WIDTH // 2
    zp = jnp.pad(cg * hv, ((0, 0), (pad, CONV_WIDTH - 1 - pad), (0, 0)))
    conv = sum(zp[:, k:k + L] * w_conv[k] for k in range(CONV_WIDTH))
    return (bg * conv) @ w_out


def chunk_gmlp_mixer(h, w_in, g_v, w_s, b_s, w_out):
    Bn, L, _ = h.shape
    u, v = jnp.split(h @ w_in, 2, axis=-1)
    v = rms_norm(v, g_v).reshape(Bn, L // CHUNK, CHUNK, GMLP_GROUPS, GMLP_GROUP_DIM)
    v = jnp.einsum('gpq,bnqgd->bnpgd', w_s, v) + b_s.T[:, :, None]
    return (u * v.reshape(Bn, L, GMLP_WIDTH)) @ w_out


def rope_tables(L):
    n_rows = L // GRID_W
    rows = jnp.broadcast_to(jnp.arange(n_rows)[:, None], (n_rows, GRID_W)).reshape(-1)
    cols = jnp.broadcast_to(jnp.arange(GRID_W)[None, :], (n_rows, GRID_W)).reshape(-1)
    nf = QK_ROPE // 4
    inv = 1.0 / (ROPE_THETA ** (jnp.arange(nf, dtype=jnp.float32) / nf))
    ang_r = rows.astype(jnp.float32)[:, None] * inv
    ang_c = cols.astype(jnp.float32)[:, None] * inv
    return (jnp.cos(ang_r), jnp.sin(ang_r), jnp.cos(ang_c), jnp.sin(ang_c))


def rotate(x, cos, sin):
    x1, x2 = jnp.split(x, 2, axis=-1)
    return jnp.concatenate([x1 * cos - x2 * sin, x1 * sin + x2 * cos], axis=-1)


def apply_rope_2d(x, tables):
    cr, sr, cc, sc = [t.reshape((t.shape[0],) + (1,) * (x.ndim - 3) + (t.shape[1],)).astype(x.dtype) for t in tables]
    xr, xc = jnp.split(x, 2, axis=-1)
    return jnp.concatenate([rotate(xr, cr, sr), rotate(xc, cc, sc)], axis=-1)


def mla_project(h, w_q_a, g_q, w_q_b, w_kv_a, g_kv):
    Bn, L, _ = h.shape
    q = (rms_norm(h @ w_q_a, g_q) @ w_q_b).reshape(Bn, L, N_HEADS, QK_NOPE + QK_ROPE)
    kv = h @ w_kv_a
    ckv = rms_norm(kv[..., :KV_RANK], g_kv)
    kpe = kv[..., KV_RANK:]
    return q[..., :QK_NOPE], q[..., QK_NOPE:], ckv, kpe


def mla_attend(q_nope, q_pe, ckv, kpe, w_kv_b, w_o):
    Bn, Lq = q_nope.shape[:2]
    Lk = ckv.shape[1]
    kv = (ckv @ w_kv_b).reshape(Bn, Lk, N_HEADS, QK_NOPE + V_DIM)
    k_nope, v = kv[..., :QK_NOPE], kv[..., QK_NOPE:]
    nb = Lq // Q_BLOCK
    scale = (QK_NOPE + QK_ROPE) ** -0.5

    def to_blocks(t):
        return jnp.moveaxis(t.reshape((Bn, nb, Q_BLOCK) + t.shape[2:]), 1, 0)

    def block(args):
        qn, qp = args
        s = jnp.einsum('bqhd,bkhd->bhqk', qn, k_nope) + jnp.einsum('bqhr,bkr->bhqk', qp, kpe)
        p = jax.nn.softmax(s.astype(jnp.float32) * scale, axis=-1).astype(v.dtype)
        return jnp.einsum('bhqk,bkhd->bqhd', p, v)

    o = lax.map(block, (to_blocks(q_nope), to_blocks(q_pe)))
    o = jnp.moveaxis(o, 0, 1).reshape(Bn, Lq, N_HEADS * V_DIM)
    return o @ w_o


def sq_relu_mlp(h, w1, w2):
    return jnp.square(jax.nn.relu(h @ w1)) @ w2


def setup_inputs(seed: int = 0) -> dict:
    key = jax.random.key(seed)
    ks = iter(jax.random.split(key, 32))

    def nrm(shape, scale=1.0):
        return jax.random.normal(next(ks), shape, jnp.float32) * scale

    D = D_MODEL
    return {
        'x_prompt': nrm((BATCH, SEQ, D)),
        'x_sample': nrm((DEC_BATCH, DEC_SEQ, D)),
        'cache_ckv': nrm((DEC_BATCH, N_MLA_LAYERS, PAST_LEN, KV_RANK)),
        'cache_kpe': nrm((DEC_BATCH, N_MLA_LAYERS, PAST_LEN, QK_ROPE)),
        'c': nrm((DEC_BATCH, D)),
        'c_ctx': nrm((D,)),
        'ada_w': nrm((DEPTH, D, N_MOD * D), 0.5 * D ** -0.5),
        'ada_b': nrm((DEPTH, N_MOD * D), 0.02),
        'norm1': 1.0 + nrm((DEPTH, D), 0.02),
        'norm2': 1.0 + nrm((DEPTH, D), 0.02),
        'conv_w_in': nrm((N_CONV_LAYERS, D, 3 * D), D ** -0.5),
        'conv_w': nrm((N_CONV_LAYERS, CONV_WIDTH, D), 0.5),
        'conv_w_out': nrm((N_CONV_LAYERS, D, D), D ** -0.5),
        'gmlp_w_in': nrm((N_GMLP_LAYERS, D, 2 * GMLP_WIDTH), D ** -0.5),
        'gmlp_g_v': 1.0 + nrm((N_GMLP_LAYERS, GMLP_WIDTH), 0.02),
        'gmlp_w_s': nrm((N_GMLP_LAYERS, GMLP_GROUPS, CHUNK, CHUNK), CHUNK ** -0.5),
        'gmlp_b_s': 1.0 + nrm((N_GMLP_LAYERS, GMLP_GROUPS, CHUNK), 0.1),
        'gmlp_w_out': nrm((N_GMLP_LAYERS, GMLP_WIDTH, D), GMLP_WIDTH ** -0.5),
        'mla_w_q_a': nrm((N_MLA_LAYERS, D, Q_RANK), D ** -0.5),
        'mla_g_q': 1.0 + nrm((N_MLA_LAYERS, Q_RANK), 0.02),
        'mla_w_q_b': nrm((N_MLA_LAYERS, Q_RANK, N_HEADS * (QK_NOPE + QK_ROPE)), Q_RANK ** -0.5),
        'mla_w_kv_a': nrm((N_MLA_LAYERS, D, KV_RANK + QK_ROPE), D ** -0.5),
        'mla_g_kv': 1.0 + nrm((N_MLA_LAYERS, KV_RANK), 0.02),
        'mla_w_kv_b': nrm((N_MLA_LAYERS, KV_RANK, N_HEADS * (QK_NOPE + V_DIM)), KV_RANK ** -0.5),
        'mla_w_o': nrm((N_MLA_LAYERS, N_HEADS * V_DIM, D), (N_HEADS * V_DIM) ** -0.5),
        'mlp_w1': nrm((DEPTH, D, D_FF), D ** -0.5),
        'mlp_w2': nrm((DEPTH, D_FF, D), 0.7 * D_FF ** -0.5),
        'final_norm': 1.0 + nrm((D,), 0.02),
    }


def reference(x_prompt, x_sample, cache_ckv, cache_kpe, c, c_ctx, ada_w, ada_b, norm1, norm2,
              conv_w_in, conv_w, conv_w_out, gmlp_w_in, gmlp_g_v, gmlp_w_s, gmlp_b_s, gmlp_w_out,
              mla_w_q_a, mla_g_q, mla_w_q_b, mla_w_kv_a, mla_g_kv, mla_w_kv_b, mla_w_o,
              mlp_w1, mlp_w2, final_norm):
    rope = rope_tables(x_sample.shape[1])
    xp, xs = x_prompt, x_sample
    new_ckv, new_kpe = [], []
    for i in range(DEPTH):
        mod_p = (jax.nn.silu(c_ctx) @ ada_w[i] + ada_b[i])[None, None, :]
        mod_s = (jax.nn.silu(c) @ ada_w[i] + ada_b[i])[:, None, :]
        sh1p, sc1p, g1p, sh2p, sc2p, g2p = jnp.split(mod_p, N_MOD, axis=-1)
        sh1s, sc1s, g1s, sh2s, sc2s, g2s = jnp.split(mod_s, N_MOD, axis=-1)
        hp = modulate(rms_norm(xp, norm1[i]), sh1p, sc1p)
        hs = modulate(rms_norm(xs, norm1[i]), sh1s, sc1s)
        kind, j = i % N_MIXERS, i // N_MIXERS
        if kind == 0:
            yp = short_conv_mixer(hp, conv_w_in[j], conv_w[j], conv_w_out[j])
            ys = short_conv_mixer(hs, conv_w_in[j], conv_w[j], conv_w_out[j])
        elif kind == 1:
            yp = chunk_gmlp_mixer(hp, gmlp_w_in[j], gmlp_g_v[j], gmlp_w_s[j], gmlp_b_s[j], gmlp_w_out[j])
            ys = chunk_gmlp_mixer(hs, gmlp_w_in[j], gmlp_g_v[j], gmlp_w_s[j], gmlp_b_s[j], gmlp_w_out[j])
        else:
            qn_p, qp_p, ckv_p, kpe_p = mla_project(hp, mla_w_q_a[j], mla_g_q[j], mla_w_q_b[j], mla_w_kv_a[j], mla_g_kv[j])
            yp = mla_attend(qn_p, qp_p, ckv_p, kpe_p, mla_w_kv_b[j], mla_w_o[j])
            new_ckv.append(ckv_p)
            new_kpe.append(kpe_p)
            qn_s, qp_s, ckv_s, kpe_s = mla_project(hs, mla_w_q_a[j], mla_g_q[j], mla_w_q_b[j], mla_w_kv_a[j], mla_g_kv[j])
            qp_s = apply_rope_2d(qp_s, rope)
            kpe_s = apply_rope_2d(kpe_s, rope)
            ckv_all = jnp.concatenate([cache_ckv[:, j], ckv_s], axis=1)
            kpe_all = jnp.concatenate([cache_kpe[:, j], kpe_s], axis=1)
            ys = mla_attend(qn_s, qp_s, ckv_all, kpe_all, mla_w_kv_b[j], mla_w_o[j])
        xp = xp + g1p * yp
        xs = xs + g1s * ys
        hp = modulate(rms_norm(xp, norm2[i]), sh2p, sc2p)
        hs = modulate(rms_norm(xs, norm2[i]), sh2s, sc2s)
        xp = xp + g2p * sq_relu_mlp(hp, mlp_w1[i], mlp_w2[i])
        xs = xs + g2s * sq_relu_mlp(hs, mlp_w1[i], mlp_w2[i])
    y_prompt = rms_norm(xp, final_norm)
    y_sample = rms_norm(xs, final_norm)
    return (y_prompt, y_sample, jnp.stack(new_ckv, axis=1), jnp.stack(new_kpe, axis=1))
```

```python
import numpy as np
import concourse.bass as bass
import concourse.mybir as mybir
from concourse.bass_utils import run_bass_kernel_spmd

F32 = mybir.dt.float32
BF16 = mybir.dt.bfloat16
ALU = mybir.AluOpType
AF = mybir.ActivationFunctionType

D = 2048
NCH = 16
TH = 1024
TT = 512
NKEY = 2560
NSLOT = 5
EPS = 1e-6
SCALE = float(192 ** -0.5)
NEG = -30000.0


class Prog:
    def __init__(self):
        self.ops = []
        self.last_w = {}
        self.readers = {}
        self.joins = {}
        self.ch_last = {}

    def _exp(self, node):
        if node >= 0:
            return (node,)
        return self.joins[node]

    def op(self, eng, emit, reads=(), writes=(), ch=None):
        deps = set()
        for k in reads:
            w = self.last_w.get(k)
            if w is not None:
                deps.update(self._exp(w))
        for k in writes:
            w = self.last_w.get(k)
            if w is not None:
                deps.update(self._exp(w))
            for r in self.readers.get(k, ()):
                deps.update(self._exp(r))
        if ch is not None and ch in self.ch_last:
            deps.add(self.ch_last[ch])
        oid = len(self.ops)
        self.ops.append([eng, emit, deps, ch])
        for k in reads:
            self.readers.setdefault(k, []).append(oid)
        for k in writes:
            self.last_w[k] = oid
            self.readers[k] = []
        if ch is not None:
            self.ch_last[ch] = oid
        return oid

    def fence(self, from_keys, to_keys):
        s = set()
        for k in from_keys:
            w = self.last_w.get(k)
            if w is not None:
                s.update(self._exp(w))
            for r in self.readers.get(k, ()):
                s.update(self._exp(r))
        jid = -(len(self.joins) + 1)
        self.joins[jid] = tuple(s)
        for k in to_keys:
            self.last_w[k] = jid
            self.readers[k] = []

    def lower(self, nc, es):
        ops = self.ops
        n = len(ops)
        for o in ops:
            if o[0] == "pe":
                o[2] = {d for d in o[2] if ops[d][0] != "pe"}
        needed = [False] * n
        for o in ops:
            for d in o[2]:
                needed[d] = True
        sems = {}

        def sem(name):
            if name not in sems:
                sems[name] = es.enter_context(nc.semaphore("s_" + str(len(sems))))
            return sems[name]

        sig = [None] * n
        tick = {}
        chcnt = {}
        for i, o in enumerate(ops):
            if o[3] is not None:
                c = chcnt.get(o[3], 0) + 1
                chcnt[o[3]] = c
                sig[i] = (("ch",) + tuple(o[3]) if isinstance(o[3], tuple) else ("ch", o[3]), 16 * c)
            elif needed[i]:
                t = tick.get(o[0], 0) + 1
                tick[o[0]] = t
                sig[i] = (("eng", o[0]), t)
        per_eng = {"pe": [], "act": [], "dve": [], "pool": [], "sp": []}
        for i, o in enumerate(ops):
            per_eng[o[0]].append(i)
        self.stats = {k: len(v) for k, v in per_eng.items()}
        self.stats["ticks"] = dict(tick)
        self.stats["ch"] = len(chcnt)

        def run(eng_name, e):
            waited = {}
            for i in per_eng[eng_name]:
                o = ops[i]
                need = {}
                for d in o[2]:
                    s, v = sig[d]
                    if v > need.get(s, 0):
                        need[s] = v
                for s, v in need.items():
                    if v > waited.get(s, 0):
                        e.wait_ge(sem(s), v)
                        waited[s] = v
                ins = o[1](e)
                if sig[i] is not None and ins is not None:
                    s, v = sig[i]
                    ins.then_inc(sem(s), 16 if s[0] == "ch" else 1)

        for i in range(n):
            if sig[i] is not None:
                sem(sig[i][0])
        block = es.enter_context(nc.Block())

        @block.tensor
        def _(e):
            run("pe", e)

        @block.scalar
        def _(e):
            run("act", e)

        @block.vector
        def _(e):
            run("dve", e)

        @block.gpsimd
        def _(e):
            run("pool", e)

        @block.sync
        def _(e):
            run("sp", e)


def build_program(dbg=None):
    from contextlib import ExitStack
    nc = bass.Bass("TRN2", target_bir_lowering=False)
    P = Prog()
    es = ExitStack()

    def din(name, shape):
        return nc.dram_tensor(name, list(shape), F32, kind="ExternalInput").ap()

    def dout(name, shape):
        return nc.dram_tensor(name, list(shape), F32, kind="ExternalOutput").ap()

    xT = din("xT", [2, 128, NCH * TH])
    xhalo = din("xhalo", [128, 32])
    cond_d = din("cond", [128, 16])
    masks_d = din("masks", [2, 128, 2 * TH])
    rope_d = din("rope", [2, 64, 2 * TH])
    kmask_d = din("kmask", [8, NKEY])
    qmask_d = din("qmask", [2, 8, TH])
    cckv_d = din("cckv", [128, 4 * 512])
    ckpe_d = din("ckpe", [64, 512])
    adab_d = din("adab", [128, 4 * 96])
    n1_d = din("n1", [128, 64])
    n2_d = din("n2", [128, 64])
    fn_d = din("fn", [128, 16])
    convw_d = din("convw", [128, 96])
    gv_d = din("gv", [128, 16])
    bsbc_d = din("bsbc", [128, 2048])
    wsT_d = din("wsT", [128, 2048])
    gq_d = din("gq", [128, 4])
    gkv_d = din("gkv", [128, 4])
    ident_d = din("ident", [128, 128])
    w_ada = din("w_ada", [384, 128, 2048])
    w_cin = din("w_cin", [96, 128, 2048])
    w_cout = din("w_cout", [32, 128, 2048])
    w_gin = din("w_gin", [32, 128, 2048])
    w_gout = din("w_gout", [16, 128, 2048])
    w_qa = din("w_qa", [4, 128, 2048])
    w_kva = din("w_kva", [5, 128, 2048])
    w_qb = din("w_qb", [8, 128, 2048])
    w_kvb = din("w_kvb", [8, 128, 2048])
    w_o = din("w_o", [16, 128, 2048])
    w_1 = din("w_1", [256, 128, 2048])
    w_2 = din("w_2", [256, 128, 2048])
    yT = dout("yT", [2, 128, NCH * TH])
    ckvo = dout("ckvo", [128, 4 * 2048])
    kpeo = dout("kpeo", [64, 2048])
    xs = nc.dram_tensor("xs_scratch", [2, 128, NCH * TH], F32).ap()

    off = [20480]

    def sb(name, cols, dt, at=None):
        nbytes = cols * (4 if dt == F32 else 2)
        if at is None:
            o = off[0]
            off[0] = (o + nbytes + 63) // 64 * 64
        else:
            o = at
        t = nc.alloc_sbuf_tensor_at(name, [128, cols], dt, offset=o)
        return t, o

    x, _ = sb("x", NCH * TH, F32)
    h, h_off = sb("h", NCH * TH, BF16)
    z, z_off = sb("z", NCH * TH, BF16)
    zf, _ = sb("zf", NCH * TH // 2, F32, at=z_off)
    wbuf = [sb("w%d" % i, 2048, BF16)[0] for i in range(NSLOT)]
    mods, _ = sb("mods", 384, F32)
    adab, _ = sb("adab", 384, F32)
    n1, _ = sb("n1", 64, F32)
    n2, _ = sb("n2", 64, F32)
    fn, _ = sb("fn", 16, F32)
    a_sc, _ = sb("a_sc", 32, F32)
    condt, _ = sb("condt", 16, F32)
    sfm, _ = sb("sfm", 16, BF16)
    ones2048, _ = sb("ones2048", 128, BF16)
    ones512, _ = sb("ones512", 128, BF16)
    ones1, _ = sb("ones1", 128, BF16)
    ident, _ = sb("ident", 128, BF16)
    convw, _ = sb("convw", 96, F32)
    gv, _ = sb("gv", 16, F32)
    gq, _ = sb("gq", 4, F32)
    gkv, _ = sb("gkv", 4, F32)
    xcol, _ = sb("xcol", 32, F32)
    hcol, _ = sb("hcol", 16, BF16)
    usave, _ = sb("usave", 16, F32)
    colt, _ = sb("colt", 16, F32)
    colsq, _ = sb("colsq", 16, BF16)
    colr, _ = sb("colr", 4, F32)
    rstd, _ = sb("rstd", TH, F32)
    sq = [sb("sq%d" % i, TT, BF16)[0] for i in range(2)]
    xr = [sb("xr%d" % i, TT, F32)[0] for i in range(2)]
    arena = off[0]
    off[0] = arena
    ubuf = [sb("ubuf%d" % i, TH + 2, F32)[0] for i in range(2)]
    maskt, _ = sb("maskt", 2 * TH, F32)
    ct1, _ = sb("ct1", TH, F32)
    ct2, _ = sb("ct2", TH, F32)
    cacc = [sb("cacc%d" % i, TH, F32)[0] for i in range(2)]
    end_conv = off[0]
    off[0] = arena
    ckv_all, _ = sb("ckv_all", 4 * NKEY, BF16)
    kpe_all, _ = sb("kpe_all", NKEY, BF16)
    xreg = off[0]
    bsbc, _ = sb("bsbc", 2048, F32)
    wsT, _ = sb("wsT", 2048, BF16)
    vT = [sb("vT%d" % i, 128, BF16)[0] for i in range(2)]
    end_gmlp = off[0]
    off[0] = xreg
    qan, _ = sb("qan", 4 * TH, BF16)
    rec, _ = sb("rec", TT, F32)
    qpe = [sb("qpe%d" % i, TT, BF16)[0] for i in range(2)]
    end_mla = off[0]
    assert max(end_conv, end_gmlp, end_mla) <= 229376, (end_conv, end_gmlp, end_mla)
    ropeCk, _ = sb("ropeCk", TH, F32, at=z_off + 16384)
    ropeSk, _ = sb("ropeSk", TH, F32, at=z_off + 20480)
    knope, _ = sb("knope", NKEY, BF16, at=h_off)
    vh, _ = sb("vh", NKEY, BF16, at=h_off + 3 * 2048)
    pT = [sb("pT%d" % i, TT, BF16, at=h_off + 6 * 2048 + i * 1024)[0] for i in range(3)]
    qn = [sb("qn%d" % i, TT, BF16, at=h_off + 7 * 2048 + 1024 + i * 1024)[0] for i in range(2)]
    ropeC, _ = sb("ropeC", TH, F32, at=h_off + 10 * 2048)
    ropeS, _ = sb("ropeS", TH, F32, at=h_off + 12 * 2048)
    ps = [nc.alloc_psum_tensor("ps%d" % i, [128, 512], F32) for i in range(7)]
    psb = nc.alloc_psum_tensor("psb", [128, 1024], BF16)

    HKEYS = [("h", c, t) for c in range(NCH) for t in range(2)]
    TMPKEYS = ["knope", "vh", "pT0", "pT1", "pT2", "qn0", "qn1", "ropeC", "ropeS"]
    CONVK = [("ubuf", 0), ("ubuf", 1), "maskt", "ct1", "ct2", ("cacc", 0), ("cacc", 1)]
    KVK = [("ckv_all", cc) for cc in range(4)] + ["kpe_all"]
    GMLPK = ["bsbc", "wsT", ("vT", 0), ("vT", 1)]
    MLAK = [("qan", cc, t) for cc in range(4) for t in range(2)] + ["rec", ("qpe", 0), ("qpe", 1)]
    RCK = [("z", c, t) for c in (8, 9) for t in range(2)]
    RSK = [("z", c, t) for c in (10, 11) for t in range(2)]

    def zfk(cc, t):
        return [("z", 2 * cc + t, 0), ("z", 2 * cc + t, 1)]

    st = {"slot": 0, "bank": 0, "spch": 0, "xr": 0, "sq": 0}

    def wload(src_ap):
        s = st["slot"]
        st["slot"] = (s + 1) % NSLOT
        P.op("pool", lambda e, s=s, src_ap=src_ap: e.dma_start(out=wbuf[s][:, :], in_=src_ap),
             writes=[("W", s)], ch=("w", s))
        return s

    def bank(nb=5):
        b = st["bank"] % nb
        st["bank"] += 1
        return b

    def sp_dma(out_ap, in_ap, reads=(), writes=(), eng="sp", nch=4, chname="sp"):
        c = st["spch"] % nch
        st["spch"] += 1
        return P.op(eng, lambda e: e.dma_start(out=out_ap, in_=in_ap), reads=reads, writes=writes,
                    ch=(chname, c))

    def mm(out_ap, pairs, reads, writes):
        def emit(e):
            ins = None
            n = len(pairs)
            for i, (l, r) in enumerate(pairs):
                ins = e.matmul(out_ap, lhsT=l, rhs=r, start=(i == 0), stop=(i == n - 1))
            return ins
        return P.op("pe", emit, reads, writes)

    def act(out_ap, in_ap, func, reads, writes, **kw):
        return P.op("act", lambda e: e.activation(out=out_ap, in_=in_ap, func=func, **kw), reads, writes)

    def tt(out_ap, in0, in1, op, reads, writes, eng="dve"):
        return P.op(eng, lambda e: e.tensor_tensor(out=out_ap, in0=in0, in1=in1, op=op), reads, writes)

    def tsc(out_ap, in0, s1, op0, reads, writes, s2=None, op1=None, eng="dve"):
        if op1 is None:
            return P.op(eng, lambda e: e.tensor_scalar(out=out_ap, in0=in0, scalar1=s1, scalar2=None, op0=op0),
                        reads, writes)
        return P.op(eng, lambda e: e.tensor_scalar(out=out_ap, in0=in0, scalar1=s1, scalar2=s2, op0=op0, op1=op1),
                    reads, writes)

    def stt(out_ap, in0, scalar, in1, op0, op1, reads, writes, eng="dve"):
        return P.op(eng, lambda e: e.scalar_tensor_tensor(out=out_ap, in0=in0, scalar=scalar, in1=in1,
                                                          op0=op0, op1=op1), reads, writes)

    def recip(out_ap, in_ap, reads, writes):
        return P.op("dve", lambda e: e.reciprocal(out=out_ap, in_=in_ap), reads, writes)

    def xk(c, t):
        return ("x", c, t)

    def xa(c, t):
        return x[:, c * TH + t * TT: c * TH + (t + 1) * TT]

    def ha(c, t):
        return h[:, c * TH + t * TT: c * TH + (t + 1) * TT]

    def za(c, t):
        return z[:, c * TH + t * TT: c * TH + (t + 1) * TT]

    def wk(s, kc, lo=0, hi=128, KC=16):
        w = 2048 // KC
        return wbuf[s][:, kc * w + lo: kc * w + hi]

    for dst, src, key in [(condt, cond_d, "cond"), (adab, adab_d, "adab"), (n1, n1_d, "n1"), (n2, n2_d, "n2"),
                          (fn, fn_d, "fn"), (convw, convw_d, "convw"), (gv, gv_d, "gv"), (gq, gq_d, "gq"),
                          (gkv, gkv_d, "gkv"), (xcol, xhalo, "xcol")]:
        sp_dma(dst[:, :], src[:, :], writes=[key])
    P.op("pool", lambda e: e.dma_start(out=ident[:, :], in_=ident_d[:, :]), writes=["ident"], ch=("w", "i"))
    P.op("dve", lambda e: e.memset(ones2048[:, :], 1.0 / 2048.0), writes=["ones2048"])
    P.op("dve", lambda e: e.memset(ones512[:, :], 1.0 / 512.0), writes=["ones512"])
    P.op("dve", lambda e: e.memset(ones1[:, :], 1.0), writes=["ones1"])
    for i in range(2):
        P.op("dve", lambda e, i=i: e.memset(ubuf[i][:, :], 0.0), writes=[("ubuf", i)])

    act(sfm[:, :], condt[:, :], AF.Silu, ["cond"], ["sfm"])
    NL = 4 if dbg is None else dbg.get("nl", 4)
    for l in range(NL):
        for j in range(96):
            s = wload(w_ada[l * 96 + j])
            mm(ps[6][:, j:j + 1], [(wk(s, kc), sfm[:, kc:kc + 1]) for kc in range(16)],
               [("W", s), "sfm"], [("ps", 6)])
        tt(mods[:, l * 96:(l + 1) * 96], ps[6][:, 0:96], adab[:, l * 96:(l + 1) * 96], ALU.add,
           [("ps", 6), "adab"], [("mods", l)])

    def mod(l, i):
        return mods[:, l * 96 + i * 16: l * 96 + (i + 1) * 16]

    def layer_scales(l):
        stt(a_sc[:, 0:16], mod(l, 1), 1.0, n1[:, l * 16:(l + 1) * 16], ALU.add, ALU.mult,
            [("mods", l), "n1"], ["a1"])
        stt(a_sc[:, 16:32], mod(l, 4), 1.0, n2[:, l * 16:(l + 1) * 16], ALU.add, ALU.mult,
            [("mods", l), "n2"], ["a2"])

    def load_x(half, src):
        for c in range(NCH):
            rd = [("xs", half, c)] if src is xs else []
            sp_dma(x[:, c * TH:(c + 1) * TH], src[half, :, c * TH:(c + 1) * TH], reads=rd,
                   writes=[xk(c, 0), xk(c, 1)], chname="xl")

    def store_x(half, dst, c):
        wr = [("xs", half, c)] if dst is xs else []
        sp_dma(dst[half, :, c * TH:(c + 1) * TH], x[:, c * TH:(c + 1) * TH], reads=[xk(c, 0), xk(c, 1)],
               writes=wr, chname="xst" if dst is xs else "yst")

    def norm_to_h(l, which):
        a_ap = a_sc[:, 0:16] if which == 1 else a_sc[:, 16:32]
        sh_ap = mod(l, 0) if which == 1 else mod(l, 3)
        akey = "a1" if which == 1 else "a2"
        for t in range(2):
            sb_ = 5 + t
            for c in range(NCH):
                q = st["sq"] % 2
                st["sq"] += 1
                act(sq[q][:, :], xa(c, t), AF.Square, [xk(c, t)], [("sq", q)])
                P.op("pe", lambda e, q=q, c=c, sb_=sb_: e.matmul(ps[sb_][:, :], lhsT=ones2048[:, :], rhs=sq[q][:, :],
                                                              start=(c == 0), stop=(c == NCH - 1)),
                     [("sq", q), "ones2048"], [("ps", sb_)])
            act(rstd[:, t * TT:(t + 1) * TT], ps[sb_][:, :], AF.Sqrt, [("ps", sb_)], [("rstd", t)], bias=EPS, scale=1.0)
            recip(rstd[:, t * TT:(t + 1) * TT], rstd[:, t * TT:(t + 1) * TT], [("rstd", t)], [("rstd", t)])
            for c in range(NCH):
                r = st["xr"] % 2
                st["xr"] += 1
                tt(xr[r][:, :], xa(c, t), rstd[:, t * TT:(t + 1) * TT], ALU.mult, [xk(c, t), ("rstd", t)], [("xr", r)])
                act(ha(c, t), xr[r][:, :], AF.Identity, [("xr", r), akey, ("mods", l)], [("h", c, t)],
                    scale=a_ap[:, c:c + 1], bias=sh_ap[:, c:c + 1])

    def out_proj(wsrc, blk0, src_fn, src_keys_fn, l, gate_i, after=None):
        for j in range(NCH):
            s = wload(wsrc[blk0 + j])
            for t in range(2):
                b = bank()
                mm(ps[b][:, :], [(wk(s, kc), src_fn(kc, t)) for kc in range(16)],
                   [("W", s)] + src_keys_fn(t), [("ps", b)])
                stt(xa(j, t), ps[b][:, :], mod(l, gate_i)[:, j:j + 1], xa(j, t), ALU.mult, ALU.add,
                    [("ps", b), ("mods", l), xk(j, t)], [xk(j, t)])
            if after is not None:
                after(j)

    def zsrc(kc, t):
        return za(kc, t)

    def zkeys(t):
        return [("z", c, t) for c in range(NCH)]

    def hkeys(t):
        return [("h", c, t) for c in range(NCH)]

    def mlp(l, after=None):
        for g in range(4):
            for m in range(16):
                s = wload(w_1[l * 64 + g * 16 + m])
                for t in range(2):
                    b = bank()
                    mm(ps[b][:, :], [(wk(s, kc), ha(kc, t)) for kc in range(16)], [("W", s)] + hkeys(t), [("ps", b)])
                    r = st["xr"] % 2
                    st["xr"] += 1
                    act(xr[r][:, :], ps[b][:, :], AF.Relu, [("ps", b)], [("xr", r)])
                    tt(za(m, t), xr[r][:, :], xr[r][:, :], ALU.mult, [("xr", r)], [("z", m, t)])
            out_proj(w_2, l * 64 + g * 16, zsrc, zkeys, l, 5, after=after if g == 3 else None)

    def conv_mixer(l, jl, half, first):
        halo_col = TH + 1 if half == 0 else 0
        edge_col = TH if half == 0 else 1
        sp_dma(maskt[:, :], masks_d[half, :, :], writes=["maskt"], chname="msk", nch=1)
        if first:
            xc = xcol[:, 16:32] if half == 0 else xcol[:, 0:16]
            act(colsq[:, :], xc, AF.Square, ["xcol"], ["colsq"])
            mm(ps[6][:, 0:1], [(ones2048[:, :], colsq[:, c:c + 1]) for c in range(NCH)], ["colsq", "ones2048"], [("ps", 6)])
            act(colr[:, 0:1], ps[6][:, 0:1], AF.Sqrt, [("ps", 6)], ["colr"], bias=EPS, scale=1.0)
            recip(colr[:, 1:2], colr[:, 0:1], ["colr"], ["colr2"])
            tsc(colt[:, :], xc, colr[:, 1:2], ALU.mult, ["xcol", "colr2"], ["colt"])
            tt(colt[:, :], colt[:, :], a_sc[:, 0:16], ALU.mult, ["colt", "a1"], ["colt"])
            tt(hcol[:, :], colt[:, :], mod(l, 0), ALU.add, ["colt", ("mods", l)], ["hcol"])
        for c in range(NCH):
            ub = c % 2
            U = ubuf[ub]
            s_bg = wload(w_cin[jl * 48 + c * 3 + 0])
            s_cg = wload(w_cin[jl * 48 + c * 3 + 1])
            s_hv = wload(w_cin[jl * 48 + c * 3 + 2])
            for t in range(2):
                b1 = bank()
                mm(ps[b1][:, :], [(wk(s_cg, kc), ha(kc, t)) for kc in range(16)], [("W", s_cg)] + hkeys(t), [("ps", b1)])
                b2 = bank()
                mm(ps[b2][:, :], [(wk(s_hv, kc), ha(kc, t)) for kc in range(16)], [("W", s_hv)] + hkeys(t), [("ps", b2)])
                r = st["xr"] % 2
                st["xr"] += 1
                act(xr[r][:, :], ps[b1][:, :], AF.Copy, [("ps", b1)], [("xr", r)])
                tt(U[:, 1 + t * TT: 1 + (t + 1) * TT], xr[r][:, :], ps[b2][:, :], ALU.mult,
                   [("xr", r), ("ps", b2)], [("ubuf", ub)])
            if first:
                mm(ps[6][:, 0:1], [(wk(s_cg, kc), hcol[:, kc:kc + 1]) for kc in range(16)], [("W", s_cg), "hcol"], [("ps", 6)])
                mm(ps[6][:, 1:2], [(wk(s_hv, kc), hcol[:, kc:kc + 1]) for kc in range(16)], [("W", s_hv), "hcol"], [("ps", 6)])
                act(colr[:, 2:3], ps[6][:, 0:1], AF.Copy, [("ps", 6)], ["colr3"])
                tt(U[:, halo_col:halo_col + 1], colr[:, 2:3], ps[6][:, 1:2], ALU.mult, ["colr3", ("ps", 6)], [("ubuf", ub)])
                P.op("dve", lambda e, U=U, c=c: e.tensor_copy(out=usave[:, c:c + 1], in_=U[:, edge_col:edge_col + 1]),
                     [("ubuf", ub)], [("usave", c)])
            else:
                P.op("dve", lambda e, U=U, c=c: e.tensor_copy(out=U[:, halo_col:halo_col + 1], in_=usave[:, c:c + 1]),
                     [("usave", c)], [("ubuf", ub)])
            oc = 0 if half == 0 else TH + 1
            P.op("dve", lambda e, U=U: e.memset(U[:, oc:oc + 1], 0.0), [], [("ubuf", ub)])
            A = cacc[ub]
            w0 = convw[:, jl * 48 + 0 * 16 + c: jl * 48 + 0 * 16 + c + 1]
            w1 = convw[:, jl * 48 + 1 * 16 + c: jl * 48 + 1 * 16 + c + 1]
            w2 = convw[:, jl * 48 + 2 * 16 + c: jl * 48 + 2 * 16 + c + 1]
            tt(ct1[:, :], U[:, 0:TH], maskt[:, 0:TH], ALU.mult, [("ubuf", ub), "maskt"], ["ct1"])
            tt(ct2[:, :], U[:, 2:TH + 2], maskt[:, TH:2 * TH], ALU.mult, [("ubuf", ub), "maskt"], ["ct2"])
            tsc(A[:, :], U[:, 1:TH + 1], w1, ALU.mult, [("ubuf", ub), "convw"], [("cacc", ub)])
            stt(A[:, :], ct1[:, :], w0, A[:, :], ALU.mult, ALU.add, ["ct1", "convw", ("cacc", ub)], [("cacc", ub)])
            stt(A[:, :], ct2[:, :], w2, A[:, :], ALU.mult, ALU.add, ["ct2", "convw", ("cacc", ub)], [("cacc", ub)])
            for t in range(2):
                b = bank()
                mm(ps[b][:, :], [(wk(s_bg, kc), ha(kc, t)) for kc in range(16)], [("W", s_bg)] + hkeys(t), [("ps", b)])
                tt(za(c, t), A[:, t * TT:(t + 1) * TT], ps[b][:, :], ALU.mult, [("cacc", ub), ("ps", b)], [("z", c, t)])
        out_proj(w_cout, jl * 16, zsrc, zkeys, l, 2)

    def gmlp_mixer(l):
        for c in range(NCH):
            s = wload(w_gin[c])
            for t in range(2):
                b = bank()
                mm(ps[b][:, :], [(wk(s, kc), ha(kc, t)) for kc in range(16)], [("W", s)] + hkeys(t), [("ps", b)])
                act(za(c, t), ps[b][:, :], AF.Copy, [("ps", b)], [("z", c, t)])
                q = st["sq"] % 2
                st["sq"] += 1
                act(sq[q][:, :], ps[b][:, :], AF.Square, [("ps", b)], [("sq", q)])
                P.op("pe", lambda e, q=q, c=c, t=t: e.matmul(ps[5 + t][:, :], lhsT=ones2048[:, :], rhs=sq[q][:, :],
                                                           start=(c == 0), stop=(c == NCH - 1)),
                     [("sq", q), "ones2048"], [("ps", 5 + t)])
        for t in range(2):
            act(rstd[:, t * TT:(t + 1) * TT], ps[5 + t][:, :], AF.Sqrt, [("ps", 5 + t)], [("rstd", t)], bias=EPS, scale=1.0)
            recip(rstd[:, t * TT:(t + 1) * TT], rstd[:, t * TT:(t + 1) * TT], [("rstd", t)], [("rstd", t)])
        for c in range(NCH):
            for t in range(2):
                stt(za(c, t), za(c, t), gv[:, c:c + 1], rstd[:, t * TT:(t + 1) * TT], ALU.mult, ALU.mult,
                    [("z", c, t), "gv", ("rstd", t)], [("z", c, t)])
        tcount = 0
        for c in range(NCH):
            for t in range(2):
                b = bank()
                for n in range(4):
                    sl = tcount % 8
                    tcount += 1
                    v = vT[sl % 2]
                    zin = z[:, c * TH + t * TT + n * 128: c * TH + t * TT + (n + 1) * 128]
                    P.op("pe", lambda e, sl=sl, zin=zin: e.transpose(out=psb[:, sl * 128:(sl + 1) * 128], in_=zin, identity=ident[:, :]),
                         [("z", c, t), "ident"], [("psb", sl)])
                    P.op("act", lambda e, sl=sl, v=v: e.activation(out=v[:, :], in_=psb[:, sl * 128:(sl + 1) * 128], func=AF.Copy),
                         [("psb", sl)], [("vT", sl % 2)])
                    mm(ps[b][:, n * 128:(n + 1) * 128], [(v[:, :], wsT[:, c * 128:(c + 1) * 128])],
                       [("vT", sl % 2), "wsT"], [("ps", b)])
                for n in range(4):
                    tt(z[:, c * TH + t * TT + n * 128: c * TH + t * TT + (n + 1) * 128], ps[b][:, n * 128:(n + 1) * 128],
                       bsbc[:, c * 128:(c + 1) * 128], ALU.add, [("ps", b), "bsbc"], [("z", c, t)])
        for c in range(NCH):
            s = wload(w_gin[16 + c])
            for t in range(2):
                b = bank()
                mm(ps[b][:, :], [(wk(s, kc), ha(kc, t)) for kc in range(16)], [("W", s)] + hkeys(t), [("ps", b)])
                tt(za(c, t), ps[b][:, :], za(c, t), ALU.mult, [("ps", b), ("z", c, t)], [("z", c, t)])
        out_proj(w_gout, 0, zsrc, zkeys, l, 2)

    def mla_kside(half):
        sp_dma(ropeCk[0:64, :], rope_d[half, :, 0:TH], writes=RCK, chname="rp", nch=2)
        sp_dma(ropeSk[0:64, :], rope_d[half, :, TH:2 * TH], writes=RSK, chname="rp", nch=2)
        for cc in range(4):
            s = wload(w_kva[cc])
            for t in range(2):
                b = bank()
                mm(ps[b][:, :], [(wk(s, kc), ha(kc, t)) for kc in range(16)], [("W", s)] + hkeys(t), [("ps", b)])
                act(zf[:, cc * TH + t * TT: cc * TH + (t + 1) * TT], ps[b][:, :], AF.Copy, [("ps", b)], zfk(cc, t))
                q = st["sq"] % 2
                st["sq"] += 1
                act(sq[q][:, :], ps[b][:, :], AF.Square, [("ps", b)], [("sq", q)])
                P.op("pe", lambda e, q=q, cc=cc, t=t: e.matmul(ps[5 + t][:, :], lhsT=ones512[:, :], rhs=sq[q][:, :],
                                                             start=(cc == 0), stop=(cc == 3)),
                     [("sq", q), "ones512"], [("ps", 5 + t)])
        for t in range(2):
            act(rstd[:, t * TT:(t + 1) * TT], ps[5 + t][:, :], AF.Sqrt, [("ps", 5 + t)], [("rstd", t)], bias=EPS, scale=1.0)
            recip(rstd[:, t * TT:(t + 1) * TT], rstd[:, t * TT:(t + 1) * TT], [("rstd", t)], [("rstd", t)])
        for cc in range(4):
            for t in range(2):
                zz = zf[:, cc * TH + t * TT: cc * TH + (t + 1) * TT]
                stt(zz, zz, gkv[:, cc:cc + 1], rstd[:, t * TT:(t + 1) * TT], ALU.mult, ALU.mult,
                    zfk(cc, t) + ["gkv", ("rstd", t)], zfk(cc, t))
                tok = half * TH + t * TT
                sp_dma(ckvo[:, cc * 2048 + tok: cc * 2048 + tok + TT], zz, reads=zfk(cc, t), chname="ko", nch=2)
                act(ckv_all[:, cc * NKEY + 512 + tok: cc * NKEY + 512 + tok + TT], zz, AF.Copy,
                    zfk(cc, t), [("ckv_all", cc)])
        s = wload(w_kva[4])
        for t in range(2):
            b1 = bank()
            mm(ps[b1][0:64, :], [(wk(s, kc, 0, 64), ha(kc, t)) for kc in range(16)], [("W", s)] + hkeys(t), [("ps", b1)])
            b2 = bank()
            mm(ps[b2][0:64, :], [(wk(s, kc, 64, 128), ha(kc, t)) for kc in range(16)], [("W", s)] + hkeys(t), [("ps", b2)])
            r1 = st["xr"] % 2
            st["xr"] += 1
            r2 = st["xr"] % 2
            st["xr"] += 1
            tt(xr[r1][0:64, :], ps[b1][0:64, :], ropeCk[0:64, t * TT:(t + 1) * TT], ALU.mult, [("ps", b1)] + RCK, [("xr", r1)])
            tt(xr[r2][0:64, :], ps[b2][0:64, :], ropeSk[0:64, t * TT:(t + 1) * TT], ALU.mult, [("ps", b2)] + RSK, [("xr", r2)])
            tt(xr[r1][0:64, :], xr[r1][0:64, :], xr[r2][0:64, :], ALU.add, [("xr", r1), ("xr", r2)], [("xr", r1)])
            tok = half * TH + t * TT
            sp_dma(kpeo[:, tok:tok + TT], xr[r1][0:64, :], reads=[("xr", r1)], chname="ko", nch=2)
            act(kpe_all[0:64, 512 + tok: 512 + tok + TT], xr[r1][0:64, :], AF.Copy, [("xr", r1)], ["kpe_all"])

    def mla_attend(l, half, rope_loaded):
        P.op("pool", lambda e: e.dma_start(out=qpe[0][64:72, :], in_=qmask_d[half, :, 0:TT]), writes=[("qpe", 0)], ch=("w", "q"))
        P.op("pool", lambda e: e.dma_start(out=qpe[1][64:72, :], in_=qmask_d[half, :, TT:2 * TT]), writes=[("qpe", 1)], ch=("w", "q"))
        for cc in range(4):
            s = wload(w_qa[cc])
            for t in range(2):
                b = bank()
                mm(ps[b][:, :], [(wk(s, kc), ha(kc, t)) for kc in range(16)], [("W", s)] + hkeys(t), [("ps", b)])
                act(qan[:, cc * TH + t * TT: cc * TH + (t + 1) * TT], ps[b][:, :], AF.Copy, [("ps", b)], [("qan", cc, t)])
                q = st["sq"] % 2
                st["sq"] += 1
                act(sq[q][:, :], ps[b][:, :], AF.Square, [("ps", b)], [("sq", q)])
                P.op("pe", lambda e, q=q, cc=cc, t=t: e.matmul(ps[5 + t][:, :], lhsT=ones512[:, :], rhs=sq[q][:, :],
                                                             start=(cc == 0), stop=(cc == 3)),
                     [("sq", q), "ones512"], [("ps", 5 + t)])
        for t in range(2):
            act(rstd[:, t * TT:(t + 1) * TT], ps[5 + t][:, :], AF.Sqrt, [("ps", 5 + t)], [("rstd", t)], bias=EPS, scale=1.0)
            recip(rstd[:, t * TT:(t + 1) * TT], rstd[:, t * TT:(t + 1) * TT], [("rstd", t)], [("rstd", t)])
        for cc in range(4):
            for t in range(2):
                qq = qan[:, cc * TH + t * TT: cc * TH + (t + 1) * TT]
                stt(qq, qq, gq[:, cc:cc + 1], rstd[:, t * TT:(t + 1) * TT], ALU.mult, ALU.mult,
                    [("qan", cc, t), "gq", ("rstd", t)], [("qan", cc, t)])
        P.fence(HKEYS, TMPKEYS)
        if not rope_loaded:
            sp_dma(ropeC[0:64, :], rope_d[half, :, 0:TH], writes=["ropeC"], chname="rp", nch=2)
            sp_dma(ropeS[0:64, :], rope_d[half, :, TH:2 * TH], writes=["ropeS"], chname="rp", nch=2)
        qanK = [("qan", cc, t) for cc in range(4) for t in range(2)]
        ckvK = [("ckv_all", cc) for cc in range(4)]
        pcount = 0
        for hd in range(16):
            if hd % 2 == 0:
                s_qb = wload(w_qb[hd // 2])
                s_kvb = wload(w_kvb[hd // 2])
            ho = (hd % 2) * 256
            for k5 in range(5):
                b = 5 + (st["bank"] % 2)
                st["bank"] += 1
                mm(ps[b][:, :], [(wk(s_kvb, kc, ho, ho + 128, KC=4), ckv_all[:, kc * NKEY + k5 * 512: kc * NKEY + (k5 + 1) * 512])
                                 for kc in range(4)], [("W", s_kvb)] + ckvK, [("ps", b)])
                act(knope[:, k5 * 512:(k5 + 1) * 512], ps[b][:, :], AF.Copy, [("ps", b)], ["knope"])
            for k5 in range(5):
                b = 5 + (st["bank"] % 2)
                st["bank"] += 1
                for n in range(4):
                    kt = k5 * 4 + n
                    mm(ps[b][:, n * 128:(n + 1) * 128],
                       [(ckv_all[:, kc * NKEY + kt * 128: kc * NKEY + (kt + 1) * 128], wk(s_kvb, kc, ho + 128, ho + 256, KC=4))
                        for kc in range(4)], [("W", s_kvb)] + ckvK, [("ps", b)])
                P.op("dve", lambda e, b=b, k5=k5: e.tensor_copy(out=vh[:, k5 * 512:(k5 + 1) * 512], in_=ps[b][:, :]),
                     [("ps", b)], ["vh"])
            for qt in range(2):
                b = 5 + (st["bank"] % 2)
                st["bank"] += 1
                qq = st["xr"] % 2
                mm(ps[b][:, :], [(wk(s_qb, kc, ho, ho + 128, KC=4), qan[:, kc * TH + qt * TT: kc * TH + (qt + 1) * TT])
                                 for kc in range(4)], [("W", s_qb)] + qanK, [("ps", b)])
                qb_ = (hd * 2 + qt) % 2
                act(qn[qb_][:, :], ps[b][:, :], AF.Copy, [("ps", b)], [("qn%d" % qb_)])
                b1 = 5 + (st["bank"] % 2)
                st["bank"] += 1
                mm(ps[b1][0:64, :], [(wk(s_qb, kc, ho + 128, ho + 192, KC=4), qan[:, kc * TH + qt * TT: kc * TH + (qt + 1) * TT])
                                     for kc in range(4)], [("W", s_qb)] + qanK, [("ps", b1)])
                b2 = 5 + (st["bank"] % 2)
                st["bank"] += 1
                mm(ps[b2][0:64, :], [(wk(s_qb, kc, ho + 192, ho + 256, KC=4), qan[:, kc * TH + qt * TT: kc * TH + (qt + 1) * TT])
                                     for kc in range(4)], [("W", s_qb)] + qanK, [("ps", b2)])
                r1 = st["xr"] % 2
                st["xr"] += 1
                r2 = st["xr"] % 2
                st["xr"] += 1
                tt(xr[r1][0:64, :], ps[b1][0:64, :], ropeC[0:64, qt * TT:(qt + 1) * TT], ALU.mult, [("ps", b1), "ropeC"], [("xr", r1)])
                tt(xr[r2][0:64, :], ps[b2][0:64, :], ropeS[0:64, qt * TT:(qt + 1) * TT], ALU.mult, [("ps", b2), "ropeS"], [("xr", r2)])
                tt(qpe[qt][0:64, :], xr[r1][0:64, :], xr[r2][0:64, :], ALU.add, [("xr", r1), ("xr", r2)], [("qpe", qt)])
                def score(kt):
                    sbk = kt % 3
                    mm(ps[sbk][:, :], [(knope[:, kt * 128:(kt + 1) * 128], qn[qb_][:, :]),
                                       (kpe_all[0:72, kt * 128:(kt + 1) * 128], qpe[qt][0:72, :])],
                       ["knope", "qn%d" % qb_, "kpe_all", ("qpe", qt)], [("ps", sbk)])
                score(0)
                for kt in range(20):
                    if kt + 1 < 20:
                        score(kt + 1)
                    sbk = kt % 3
                    pb = pcount % 3
                    pcount += 1
                    act(pT[pb][:, :], ps[sbk][:, :], AF.Exp, [("ps", sbk)], ["pT%d" % pb], scale=SCALE)
                    P.op("pe", lambda e, kt=kt, pb=pb: e.matmul(ps[3][:, :], lhsT=vh[:, kt * 128:(kt + 1) * 128], rhs=pT[pb][:, :],
                                                              start=(kt == 0), stop=(kt == 19)),
                         ["vh", "pT%d" % pb], [("ps", 3)])
                    P.op("pe", lambda e, kt=kt, pb=pb: e.matmul(ps[4][:, :], lhsT=ones1[:, :], rhs=pT[pb][:, :],
                                                              start=(kt == 0), stop=(kt == 19)),
                         ["ones1", "pT%d" % pb], [("ps", 4)])
                recip(rec[:, :], ps[4][:, :], [("ps", 4)], ["rec"])
                tt(za(hd, qt), ps[3][:, :], rec[:, :], ALU.mult, [("ps", 3), "rec"], [("z", hd, qt)])
        out_proj(w_o, 0, zsrc, zkeys, l, 2)
        P.fence(TMPKEYS, HKEYS)

    def final_out(half):
        for t in range(2):
            sb_ = 5 + t
            for c in range(NCH):
                q = st["sq"] % 2
                st["sq"] += 1
                act(sq[q][:, :], xa(c, t), AF.Square, [xk(c, t)], [("sq", q)])
                P.op("pe", lambda e, q=q, c=c, sb_=sb_: e.matmul(ps[sb_][:, :], lhsT=ones2048[:, :], rhs=sq[q][:, :],
                                                              start=(c == 0), stop=(c == NCH - 1)),
                     [("sq", q), "ones2048"], [("ps", sb_)])
            act(rstd[:, t * TT:(t + 1) * TT], ps[sb_][:, :], AF.Sqrt, [("ps", sb_)], [("rstd", t)], bias=EPS, scale=1.0)
            recip(rstd[:, t * TT:(t + 1) * TT], rstd[:, t * TT:(t + 1) * TT], [("rstd", t)], [("rstd", t)])
        for c in range(NCH):
            for t in range(2):
                stt(xa(c, t), xa(c, t), fn[:, c:c + 1], rstd[:, t * TT:(t + 1) * TT], ALU.mult, ALU.mult,
                    [xk(c, t), "fn", ("rstd", t)], [xk(c, t)])
            store_x(half, yT, c)

    orders = [(0, 1), (1, 0), (0, 1), (1, 0)]
    skip_mlp_last = bool(dbg and dbg.get("skip_mlp_last"))
    for l in range(NL):
        kind, jl = l % 3, l // 3
        last_layer = (l == NL - 1)
        layer_scales(l)
        if l == 1:
            P.fence(CONVK, KVK + GMLPK)
            for cc in range(4):
                P.op("pool", lambda e, cc=cc: e.dma_start(out=ckv_all[:, cc * NKEY: cc * NKEY + 512],
                                                         in_=cckv_d[:, cc * 512:(cc + 1) * 512]),
                     writes=[("ckv_all", cc)], ch=("w", "c"))
            P.op("pool", lambda e: e.dma_start(out=kpe_all[0:64, 0:512], in_=ckpe_d[:, :]), writes=["kpe_all"], ch=("w", "c"))
            P.op("pool", lambda e: e.dma_start(out=kpe_all[64:72, :], in_=kmask_d[:, :]), writes=["kpe_all"], ch=("w", "c"))
            sp_dma(bsbc[:, :], bsbc_d[:, :], writes=["bsbc"], chname="msk", nch=1)
            P.op("pool", lambda e: e.dma_start(out=wsT[:, :], in_=wsT_d[:, :]), writes=["wsT"], ch=("w", "i"))
        if l == 2:
            P.fence(GMLPK, MLAK)
        if l == 3:
            P.fence(KVK + MLAK, CONVK)
        for oi, half in enumerate(orders[l]):
            resident = (l > 0 and oi == 0)
            if not resident:
                load_x(half, xT if l == 0 else xs)
            if kind == 2:
                if oi == 1:
                    norm_to_h(l, 1)
                mla_attend(l, half, rope_loaded=False)
            else:
                norm_to_h(l, 1)
                if kind == 0:
                    conv_mixer(l, jl, half, first=(oi == 0))
                else:
                    gmlp_mixer(l)
            will_store = (oi == 0) and not last_layer
            need_xcol = (l + 1 < NL) and ((l + 1) % 3 == 0)
            nxt_mla = (l + 1 < NL) and ((l + 1) % 3 == 2)
            if not (last_layer and skip_mlp_last):
                norm_to_h(l, 2)

                def after_chunk(j, half=half, ws=(will_store and not nxt_mla and not need_xcol)):
                    if ws:
                        store_x(half, xs, j)
                mlp(l, after=after_chunk)
            if need_xcol:
                col = TH - 1 if half == 0 else 0
                dst = xcol[:, 0:16] if half == 0 else xcol[:, 16:32]
                P.op("dve", lambda e, col=col, dst=dst: e.tensor_copy(out=dst, in_=x[:, col:NCH * TH:TH]),
                     [xk(c, t) for c in range(NCH) for t in range(2)], ["xcol"])
            if nxt_mla:
                layer_scales(l + 1)
                norm_to_h(l + 1, 1)
                mla_kside(half)
                layer_scales(l)
            if last_layer:
                final_out(half)
            elif will_store and (nxt_mla or need_xcol):
                for j in range(NCH):
                    store_x(half, xs, j)

    P.op("sp", lambda e: None, reads=[], writes=[], ch=None)
    fin = P.ops[-1]
    for i, o in enumerate(P.ops[:-1]):
        if o[3] is not None and o[3][0] in ("yst", "ko"):
            fin[2].add(i)
    P.lower(nc, es)
    es.close()
    return nc, P


def _blk(W, KC=16):
    K, N = W.shape
    assert K == KC * 128
    ncol = 2048 // KC
    nb = N // ncol
    return np.ascontiguousarray(W.reshape(KC, 128, nb, ncol).transpose(2, 1, 0, 3)).reshape(nb, 128, KC * ncol)


def _fm(v):
    v = np.asarray(v, np.float32)
    lead = int(np.prod(v.shape[:-1])) if v.ndim > 1 else 1
    C = v.shape[-1] // 128
    return np.ascontiguousarray(v.reshape(lead, C, 128).transpose(2, 0, 1)).reshape(128, lead * C)


def _swap_cols(W):
    idx = np.concatenate([np.arange(16, 32), np.arange(0, 16), np.arange(48, 64), np.arange(32, 48)])
    return W[..., idx]


def _rope_tables():
    L = 2048
    rows = np.repeat(np.arange(L // 64), 64).astype(np.float32)
    cols = np.tile(np.arange(64), L // 64).astype(np.float32)
    inv = (1.0 / (10000.0 ** (np.arange(16, dtype=np.float32) / 16))).astype(np.float32)
    ar = rows[:, None] * inv
    ac = cols[:, None] * inv
    C = np.concatenate([np.cos(ar), np.cos(ar), np.cos(ac), np.cos(ac)], 1)
    S = np.concatenate([-np.sin(ar), np.sin(ar), -np.sin(ac), np.sin(ac)], 1)
    return C.astype(np.float32), S.astype(np.float32)


def _prep_shared(inp):
    g = {k: np.asarray(v, np.float32) for k, v in inp.items()}
    sh = {}
    sh["adab"] = _fm(g["ada_b"])
    sh["n1"] = _fm(g["norm1"])
    sh["n2"] = _fm(g["norm2"])
    sh["fn"] = _fm(g["final_norm"])
    sh["convw"] = _fm(g["conv_w"])
    sh["gv"] = _fm(g["gmlp_g_v"][0])
    sh["bsbc"] = np.ascontiguousarray(np.broadcast_to(g["gmlp_b_s"][0].reshape(1, 2048), (128, 2048)))
    sh["wsT"] = np.ascontiguousarray(g["gmlp_w_s"][0].transpose(2, 0, 1)).reshape(128, 2048)
    sh["gq"] = _fm(g["mla_g_q"][0])
    sh["gkv"] = _fm(g["mla_g_kv"][0])
    sh["ident"] = np.eye(128, dtype=np.float32)
    sh["w_ada"] = _blk(g["ada_w"].transpose(1, 0, 2).reshape(2048, 4 * 12288)) if False else \
        np.concatenate([_blk(g["ada_w"][l]) for l in range(4)], 0)
    cin = []
    for jl in range(2):
        W = g["conv_w_in"][jl]
        b = _blk(W)
        order = [k * 16 + c for c in range(16) for k in range(3)]
        cin.append(b[order])
    sh["w_cin"] = np.concatenate(cin, 0)
    sh["w_cout"] = np.concatenate([_blk(g["conv_w_out"][jl]) for jl in range(2)], 0)
    gb = _blk(g["gmlp_w_in"][0])
    sh["w_gin"] = np.concatenate([gb[16:32], gb[0:16]], 0)
    sh["w_gout"] = _blk(g["gmlp_w_out"][0])
    sh["w_qa"] = _blk(g["mla_w_q_a"][0])
    kva = g["mla_w_kv_a"][0]
    kva_ext = np.concatenate([kva, _swap_cols(kva[:, 512:576])], 1)
    sh["w_kva"] = _blk(kva_ext)
    qb = g["mla_w_q_b"][0].reshape(512, 16, 192)
    qb_ext = np.concatenate([qb, _swap_cols(qb[:, :, 128:192])], 2).reshape(512, 16 * 256)
    sh["w_qb"] = _blk(qb_ext, KC=4)
    sh["w_kvb"] = _blk(g["mla_w_kv_b"][0], KC=4)
    sh["w_o"] = _blk(g["mla_w_o"][0])
    sh["w_1"] = np.concatenate([_blk(g["mlp_w1"][l]) for l in range(4)], 0)
    w2 = []
    for l in range(4):
        for gq in range(4):
            w2.append(_blk(g["mlp_w2"][l][gq * 2048:(gq + 1) * 2048]))
    sh["w_2"] = np.concatenate(w2, 0)
    return sh


def _prep_core(inp, r):
    g = inp
    pc = {}
    if r < 4:
        X = np.asarray(g["x_prompt"][8 * r: 8 * r + 8], np.float32).reshape(2048, D)
        cond = np.asarray(g["c_ctx"], np.float32)
    else:
        X = np.asarray(g["x_sample"][r - 4], np.float32)
        cond = np.asarray(g["c"][r - 4], np.float32)
    XT = X.reshape(2, TH, NCH, 128).transpose(0, 3, 2, 1)
    pc["xT"] = np.ascontiguousarray(XT).reshape(2, 128, NCH * TH)
    hal = np.stack([X[TH - 1], X[TH]], 0)
    pc["xhalo"] = _fm(hal)
    pc["cond"] = _fm(cond)
    tok = np.arange(2048)
    if r < 4:
        mL = (tok % 256 != 0).astype(np.float32)
        mR = (tok % 256 != 255).astype(np.float32)
    else:
        mL = (tok != 0).astype(np.float32)
        mR = (tok != 2047).astype(np.float32)
    m = np.stack([np.concatenate([mL[hf * TH:(hf + 1) * TH], mR[hf * TH:(hf + 1) * TH]]) for hf in range(2)], 0)
    pc["masks"] = np.ascontiguousarray(np.broadcast_to(m[:, None, :], (2, 128, 2 * TH)))
    if r < 4:
        C = np.ones((2048, 64), np.float32)
        S = np.zeros((2048, 64), np.float32)
    else:
        C, S = _rope_tables()
    pc["rope"] = np.ascontiguousarray(np.stack(
        [np.concatenate([C[hf * TH:(hf + 1) * TH].T, S[hf * TH:(hf + 1) * TH].T], 1) for hf in range(2)], 0))
    km = np.zeros((8, NKEY), np.float32)
    qm = np.zeros((2, 8, TH), np.float32)
    if r < 4:
        seq = tok // 256
        km[:, :] = NEG
        for s_ in range(8):
            km[s_, 512 + np.nonzero(seq == s_)[0]] = 0.0
        for hf in range(2):
            for t in range(TH):
                qm[hf, seq[hf * TH + t], t] = 1.0
        cck = np.zeros((128, 4 * 512), np.float32)
        ckp = np.zeros((64, 512), np.float32)
    else:
        qm[:, 0, :] = 1.0
        cc = np.asarray(g["cache_ckv"][r - 4, 0], np.float32)
        cck = np.ascontiguousarray(cc.reshape(512, 4, 128).transpose(2, 1, 0)).reshape(128, 4 * 512)
        ckp = np.ascontiguousarray(np.asarray(g["cache_kpe"][r - 4, 0], np.float32).T)
    pc["kmask"] = km
    pc["qmask"] = qm
    pc["cckv"] = cck
    pc["ckpe"] = ckp
    return pc


_CACHE = {}


def kernel(**inputs):
    sh = _prep_shared(inputs)
    in_maps = []
    for r in range(8):
        m = dict(sh)
        m.update(_prep_core(inputs, r))
        in_maps.append(m)
    if "nc" not in _CACHE:
        _CACHE["nc"] = build_program()[0]
    nc = _CACHE["nc"]
    res = run_bass_kernel_spmd(nc, in_maps, core_ids=list(range(8)))
    outs = res.results
    y = []
    for r in range(8):
        yT = np.asarray(outs[r]["yT"]).reshape(2, 128, NCH, TH)
        y.append(np.ascontiguousarray(yT.transpose(0, 3, 2, 1)).reshape(2048, D))
    y_prompt = np.stack(y[0:4], 0).reshape(32, 256, D).astype(np.float32)
    y_sample = np.stack(y[4:8], 0).reshape(4, 2048, D).astype(np.float32)
    ck = []
    kp = []
    for r in range(4):
        c = np.asarray(outs[r]["ckvo"]).reshape(128, 4, 2048)
        ck.append(np.ascontiguousarray(c.transpose(2, 1, 0)).reshape(2048, 512))
        kp.append(np.ascontiguousarray(np.asarray(outs[r]["kpeo"]).T))
    new_ckv = np.stack(ck, 0).reshape(32, 256, 512)[:, None].astype(np.float32)
    new_kpe = np.stack(kp, 0).reshape(32, 256, 64)[:, None].astype(np.float32)
    return (y_prompt, y_sample, np.ascontiguousarray(new_ckv), np.ascontiguousarray(new_kpe))
```

```python
import numpy as np
import concourse.bass as bass
import concourse.mybir as mybir
from concourse.bass_utils import run_bass_kernel_spmd

F32 = mybir.dt.float32
BF16 = mybir.dt.bfloat16
ALU = mybir.AluOpType
AF = mybir.ActivationFunctionType

D = 2048
NCH = 16
TH = 1024
TT = 512
NKEY = 2560
NSLOT = 5
EPS = 1e-6
SCALE = float(192 ** -0.5)
NEG = -30000.0


class Prog:
    def __init__(self):
        self.ops = []
        self.last_w = {}
        self.readers = {}
        self.joins = {}
        self.ch_last = {}

    def _exp(self, node):
        if node >= 0:
            return (node,)
        return self.joins[node]

    def op(self, eng, emit, reads=(), writes=(), ch=None):
        deps = set()
        for k in reads:
            w = self.last_w.get(k)
            if w is not None:
                deps.update(self._exp(w))
        for k in writes:
            w = self.last_w.get(k)
            if w is not None:
                deps.update(self._exp(w))
            for r in self.readers.get(k, ()):
                deps.update(self._exp(r))
        if ch is not None and ch in self.ch_last:
            deps.add(self.ch_last[ch])
        oid = len(self.ops)
        self.ops.append([eng, emit, deps, ch])
        for k in reads:
            self.readers.setdefault(k, []).append(oid)
        for k in writes:
            self.last_w[k] = oid
            self.readers[k] = []
        if ch is not None:
            self.ch_last[ch] = oid
        return oid

    def fence(self, from_keys, to_keys):
        s = set()
        for k in from_keys:
            w = self.last_w.get(k)
            if w is not None:
                s.update(self._exp(w))
            for r in self.readers.get(k, ()):
                s.update(self._exp(r))
        jid = -(len(self.joins) + 1)
        self.joins[jid] = tuple(s)
        for k in to_keys:
            self.last_w[k] = jid
            self.readers[k] = []

    def lower(self, nc, es):
        ops = self.ops
        n = len(ops)
        for o in ops:
            if o[0] == "pe":
                o[2] = {d for d in o[2] if ops[d][0] != "pe"}
        needed = [False] * n
        for o in ops:
            for d in o[2]:
                needed[d] = True
        sems = {}

        def sem(name):
            if name not in sems:
                sems[name] = es.enter_context(nc.semaphore("s_" + str(len(sems))))
            return sems[name]

        sig = [None] * n
        tick = {}
        chcnt = {}
        for i, o in enumerate(ops):
            if o[3] is not None:
                c = chcnt.get(o[3], 0) + 1
                chcnt[o[3]] = c
                sig[i] = (("ch",) + tuple(o[3]) if isinstance(o[3], tuple) else ("ch", o[3]), 16 * c)
            elif needed[i]:
                t = tick.get(o[0], 0) + 1
                tick[o[0]] = t
                sig[i] = (("eng", o[0]), t)
        per_eng = {"pe": [], "act": [], "dve": [], "pool": [], "sp": []}
        for i, o in enumerate(ops):
            per_eng[o[0]].append(i)
        self.stats = {k: len(v) for k, v in per_eng.items()}
        self.stats["ticks"] = dict(tick)
        self.stats["ch"] = len(chcnt)

        def run(eng_name, e):
            waited = {}
            for i in per_eng[eng_name]:
                o = ops[i]
                need = {}
                for d in o[2]:
                    s, v = sig[d]
                    if v > need.get(s, 0):
                        need[s] = v
                for s, v in need.items():
                    if v > waited.get(s, 0):
                        e.wait_ge(sem(s), v)
                        waited[s] = v
                ins = o[1](e)
                if sig[i] is not None and ins is not None:
                    s, v = sig[i]
                    ins.then_inc(sem(s), 16 if s[0] == "ch" else 1)

        for i in range(n):
            if sig[i] is not None:
                sem(sig[i][0])
        block = es.enter_context(nc.Block())

        @block.tensor
        def _(e):
            run("pe", e)

        @block.scalar
        def _(e):
            run("act", e)

        @block.vector
        def _(e):
            run("dve", e)

        @block.gpsimd
        def _(e):
            run("pool", e)

        @block.sync
        def _(e):
            run("sp", e)


def build_program(dbg=None):
    from contextlib import ExitStack
    nc = bass.Bass("TRN2", target_bir_lowering=False)
    P = Prog()
    es = ExitStack()

    def din(name, shape):
        return nc.dram_tensor(name, list(shape), F32, kind="ExternalInput").ap()

    def dout(name, shape):
        return nc.dram_tensor(name, list(shape), F32, kind="ExternalOutput").ap()

    xT = din("xT", [2, 128, NCH * TH])
    xhalo = din("xhalo", [128, 32])
    cond_d = din("cond", [128, 16])
    masks_d = din("masks", [2, 128, 2 * TH])
    rope_d = din("rope", [2, 64, 2 * TH])
    kmask_d = din("kmask", [8, NKEY])
    qmask_d = din("qmask", [2, 8, TH])
    cckv_d = din("cckv", [128, 4 * 512])
    ckpe_d = din("ckpe", [64, 512])
    adab_d = din("adab", [128, 4 * 96])
    n1_d = din("n1", [128, 64])
    n2_d = din("n2", [128, 64])
    fn_d = din("fn", [128, 16])
    convw_d = din("convw", [128, 96])
    gv_d = din("gv", [128, 16])
    bsbc_d = din("bsbc", [128, 2048])
    wsT_d = din("wsT", [128, 2048])
    gq_d = din("gq", [128, 4])
    gkv_d = din("gkv", [128, 4])
    ident_d = din("ident", [128, 128])
    w_ada = din("w_ada", [384, 128, 2048])
    w_cin = din("w_cin", [96, 128, 2048])
    w_cout = din("w_cout", [32, 128, 2048])
    w_gin = din("w_gin", [32, 128, 2048])
    w_gout = din("w_gout", [16, 128, 2048])
    w_qa = din("w_qa", [4, 128, 2048])
    w_kva = din("w_kva", [5, 128, 2048])
    w_qb = din("w_qb", [8, 128, 2048])
    w_kvb = din("w_kvb", [8, 128, 2048])
    w_o = din("w_o", [16, 128, 2048])
    w_1 = din("w_1", [256, 128, 2048])
    w_2 = din("w_2", [256, 128, 2048])
    yT = dout("yT", [2, 128, NCH * TH])
    ckvo = dout("ckvo", [128, 4 * 2048])
    kpeo = dout("kpeo", [64, 2048])
    xs = nc.dram_tensor("xs_scratch", [2, 128, NCH * TH], F32).ap()

    off = [20480]

    def sb(name, cols, dt, at=None):
        nbytes = cols * (4 if dt == F32 else 2)
        if at is None:
            o = off[0]
            off[0] = (o + nbytes + 63) // 64 * 64
        else:
            o = at
        t = nc.alloc_sbuf_tensor_at(name, [128, cols], dt, offset=o)
        return t, o

    x, _ = sb("x", NCH * TH, F32)
    h, h_off = sb("h", NCH * TH, BF16)
    z, z_off = sb("z", NCH * TH, BF16)
    zf, _ = sb("zf", NCH * TH // 2, F32, at=z_off)
    wbuf = [sb("w%d" % i, 2048, BF16)[0] for i in range(NSLOT)]
    mods, _ = sb("mods", 384, F32)
    adab, _ = sb("adab", 384, F32)
    n1, _ = sb("n1", 64, F32)
    n2, _ = sb("n2", 64, F32)
    fn, _ = sb("fn", 16, F32)
    a_sc, _ = sb("a_sc", 32, F32)
    condt, _ = sb("condt", 16, F32)
    sfm, _ = sb("sfm", 16, BF16)
    ones2048, _ = sb("ones2048", 128, BF16)
    ones512, _ = sb("ones512", 128, BF16)
    ones1, _ = sb("ones1", 128, BF16)
    ident, _ = sb("ident", 128, BF16)
    convw, _ = sb("convw", 96, F32)
    gv, _ = sb("gv", 16, F32)
    gq, _ = sb("gq", 4, F32)
    gkv, _ = sb("gkv", 4, F32)
    xcol, _ = sb("xcol", 32, F32)
    hcol, _ = sb("hcol", 16, BF16)
    usave, _ = sb("usave", 16, F32)
    colt, _ = sb("colt", 16, F32)
    colsq, _ = sb("colsq", 16, BF16)
    colr, _ = sb("colr", 4, F32)
    rstd, _ = sb("rstd", TH, F32)
    sq = [sb("sq%d" % i, TT, BF16)[0] for i in range(2)]
    xr = [sb("xr%d" % i, TT, F32)[0] for i in range(2)]
    arena = off[0]
    off[0] = arena
    ubuf = [sb("ubuf%d" % i, TH + 2, F32)[0] for i in range(2)]
    maskt, _ = sb("maskt", 2 * TH, F32)
    ct1, _ = sb("ct1", TH, F32)
    ct2, _ = sb("ct2", TH, F32)
    cacc = [sb("cacc%d" % i, TH, F32)[0] for i in range(2)]
    end_conv = off[0]
    off[0] = arena
    ckv_all, _ = sb("ckv_all", 4 * NKEY, BF16)
    kpe_all, _ = sb("kpe_all", NKEY, BF16)
    xreg = off[0]
    bsbc, _ = sb("bsbc", 2048, F32)
    wsT, _ = sb("wsT", 2048, BF16)
    vT = [sb("vT%d" % i, 128, BF16)[0] for i in range(2)]
    end_gmlp = off[0]
    off[0] = xreg
    qan, _ = sb("qan", 4 * TH, BF16)
    rec, _ = sb("rec", TT, F32)
    qpe = [sb("qpe%d" % i, TT, BF16)[0] for i in range(2)]
    end_mla = off[0]
    assert max(end_conv, end_gmlp, end_mla) <= 229376, (end_conv, end_gmlp, end_mla)
    ropeCk, _ = sb("ropeCk", TH, F32, at=z_off + 16384)
    ropeSk, _ = sb("ropeSk", TH, F32, at=z_off + 20480)
    knope, _ = sb("knope", NKEY, BF16, at=h_off)
    vh, _ = sb("vh", NKEY, BF16, at=h_off + 3 * 2048)
    pT = [sb("pT%d" % i, TT, BF16, at=h_off + 6 * 2048 + i * 1024)[0] for i in range(3)]
    qn = [sb("qn%d" % i, TT, BF16, at=h_off + 7 * 2048 + 1024 + i * 1024)[0] for i in range(2)]
    ropeC, _ = sb("ropeC", TH, F32, at=h_off + 10 * 2048)
    ropeS, _ = sb("ropeS", TH, F32, at=h_off + 12 * 2048)
    ps = [nc.alloc_psum_tensor("ps%d" % i, [128, 512], F32) for i in range(8)]
    psb = ps[7].bitcast(BF16)

    HKEYS = [("h", c, t) for c in range(NCH) for t in range(2)]
    TMPKEYS = ["knope", "vh", "pT0", "pT1", "pT2", "qn0", "qn1", "ropeC", "ropeS"]
    CONVK = [("ubuf", 0), ("ubuf", 1), "maskt", "ct1", "ct2", ("cacc", 0), ("cacc", 1)]
    KVK = [("ckv_all", cc) for cc in range(4)] + ["kpe_all"]
    GMLPK = ["bsbc", "wsT", ("vT", 0), ("vT", 1)]
    MLAK = [("qan", cc, t) for cc in range(4) for t in range(2)] + ["rec", ("qpe", 0), ("qpe", 1)]
    RCK = [("z", c, t) for c in (8, 9) for t in range(2)]
    RSK = [("z", c, t) for c in (10, 11) for t in range(2)]

    def zfk(cc, t):
        return [("z", 2 * cc + t, 0), ("z", 2 * cc + t, 1)]

    st = {"slot": 0, "bank": 0, "spch": 0, "xr": 0, "sq": 0}

    ada_q = []
    st["wl"] = 0
    st["inpump"] = False
    ADA_EVERY = 2

    def wload(src_ap):
        if not st["inpump"]:
            st["wl"] += 1
            if st["wl"] % ADA_EVERY == 0:
                ada_pump(1)
        s = st["slot"]
        st["slot"] = (s + 1) % NSLOT
        P.op("pool", lambda e, s=s, src_ap=src_ap: e.dma_start(out=wbuf[s][:, :], in_=src_ap),
             writes=[("W", s)], ch=("w", s))
        return s

    def ada_pump(n):
        st["inpump"] = True
        while n > 0 and ada_q:
            l, j = ada_q.pop(0)
            s = wload(w_ada[l * 96 + j])
            b = bank()
            mm(ps[b][:, 0:1], [(wk(s, kc), sfm[:, kc:kc + 1]) for kc in range(16)], [("W", s), "sfm"], [("ps", b)])
            tt(mods[:, l * 96 + j: l * 96 + j + 1], ps[b][:, 0:1], adab[:, l * 96 + j: l * 96 + j + 1], ALU.add,
               [("ps", b), "adab"], [("modc", l, j)])
            n -= 1
        st["inpump"] = False

    def ada_need(l, parts):
        last = (l, max(parts) * 16 + 15)
        while ada_q and (ada_q[0][0], ada_q[0][1]) <= last:
            ada_pump(1)

    def mk(l, i):
        return [("modc", l, i * 16 + c) for c in range(16)]

    def bank(nb=5):
        b = st["bank"] % nb
        st["bank"] += 1
        return b

    def sp_dma(out_ap, in_ap, reads=(), writes=(), eng="sp", nch=4, chname="sp"):
        c = st["spch"] % nch
        st["spch"] += 1
        return P.op(eng, lambda e: e.dma_start(out=out_ap, in_=in_ap), reads=reads, writes=writes,
                    ch=(chname, c))

    def mm(out_ap, pairs, reads, writes):
        def emit(e):
            ins = None
            n = len(pairs)
            for i, (l, r) in enumerate(pairs):
                ins = e.matmul(out_ap, lhsT=l, rhs=r, start=(i == 0), stop=(i == n - 1))
            return ins
        return P.op("pe", emit, reads, writes)

    def act(out_ap, in_ap, func, reads, writes, **kw):
        return P.op("act", lambda e: e.activation(out=out_ap, in_=in_ap, func=func, **kw), reads, writes)

    def tt(out_ap, in0, in1, op, reads, writes, eng="dve"):
        return P.op(eng, lambda e: e.tensor_tensor(out=out_ap, in0=in0, in1=in1, op=op), reads, writes)

    def tsc(out_ap, in0, s1, op0, reads, writes, s2=None, op1=None, eng="dve"):
        if op1 is None:
            return P.op(eng, lambda e: e.tensor_scalar(out=out_ap, in0=in0, scalar1=s1, scalar2=None, op0=op0),
                        reads, writes)
        return P.op(eng, lambda e: e.tensor_scalar(out=out_ap, in0=in0, scalar1=s1, scalar2=s2, op0=op0, op1=op1),
                    reads, writes)

    def stt(out_ap, in0, scalar, in1, op0, op1, reads, writes, eng="dve"):
        return P.op(eng, lambda e: e.scalar_tensor_tensor(out=out_ap, in0=in0, scalar=scalar, in1=in1,
                                                          op0=op0, op1=op1), reads, writes)

    def recip(out_ap, in_ap, reads, writes):
        return P.op("dve", lambda e: e.reciprocal(out=out_ap, in_=in_ap), reads, writes)

    def xk(c, t):
        return ("x", c, t)

    def xa(c, t):
        return x[:, c * TH + t * TT: c * TH + (t + 1) * TT]

    def ha(c, t):
        return h[:, c * TH + t * TT: c * TH + (t + 1) * TT]

    def za(c, t):
        return z[:, c * TH + t * TT: c * TH + (t + 1) * TT]

    def wk(s, kc, lo=0, hi=128, KC=16):
        w = 2048 // KC
        return wbuf[s][:, kc * w + lo: kc * w + hi]

    for dst, src, key in [(condt, cond_d, "cond"), (adab, adab_d, "adab"), (n1, n1_d, "n1"), (n2, n2_d, "n2"),
                          (fn, fn_d, "fn"), (convw, convw_d, "convw"), (gv, gv_d, "gv"), (gq, gq_d, "gq"),
                          (gkv, gkv_d, "gkv"), (xcol, xhalo, "xcol")]:
        sp_dma(dst[:, :], src[:, :], writes=[key])
    P.op("pool", lambda e: e.dma_start(out=ident[:, :], in_=ident_d[:, :]), writes=["ident"], ch=("w", "i"))
    P.op("dve", lambda e: e.memset(ones2048[:, :], 1.0 / 2048.0), writes=["ones2048"])
    P.op("dve", lambda e: e.memset(ones512[:, :], 1.0 / 512.0), writes=["ones512"])
    P.op("dve", lambda e: e.memset(ones1[:, :], 1.0), writes=["ones1"])
    for i in range(2):
        P.op("dve", lambda e, i=i: e.memset(ubuf[i][:, :], 0.0), writes=[("ubuf", i)])

    act(sfm[:, :], condt[:, :], AF.Silu, ["cond"], ["sfm"])
    NL = 4 if dbg is None else dbg.get("nl", 4)
    for l in range(NL):
        for j in range(96):
            ada_q.append((l, j))
    ada_pump(48)

    def mod(l, i):
        return mods[:, l * 96 + i * 16: l * 96 + (i + 1) * 16]

    def layer_scales(l):
        ada_need(l, [1])
        stt(a_sc[:, 0:16], mod(l, 1), 1.0, n1[:, l * 16:(l + 1) * 16], ALU.add, ALU.mult,
            mk(l, 1) + ["n1"], ["a1"])

    def layer_scales2(l):
        ada_need(l, [4])
        stt(a_sc[:, 16:32], mod(l, 4), 1.0, n2[:, l * 16:(l + 1) * 16], ALU.add, ALU.mult,
            mk(l, 4) + ["n2"], ["a2"])

    def load_x(half, src):
        for c in range(NCH):
            rd = [("xs", half, c)] if src is xs else []
            sp_dma(x[:, c * TH:(c + 1) * TH], src[half, :, c * TH:(c + 1) * TH], reads=rd,
                   writes=[xk(c, 0), xk(c, 1)], chname="xl")

    def store_x(half, dst, c):
        wr = [("xs", half, c)] if dst is xs else []
        sp_dma(dst[half, :, c * TH:(c + 1) * TH], x[:, c * TH:(c + 1) * TH], reads=[xk(c, 0), xk(c, 1)],
               writes=wr, chname="xst" if dst is xs else "yst")

    def norm_to_h(l, which):
        a_ap = a_sc[:, 0:16] if which == 1 else a_sc[:, 16:32]
        sh_ap = mod(l, 0) if which == 1 else mod(l, 3)
        akey = "a1" if which == 1 else "a2"
        shp = 0 if which == 1 else 3
        ada_need(l, [shp])
        for t in range(2):
            sb_ = 5 + t
            for c in range(NCH):
                q = st["sq"] % 2
                st["sq"] += 1
                act(sq[q][:, :], xa(c, t), AF.Square, [xk(c, t)], [("sq", q)])
                P.op("pe", lambda e, q=q, c=c, sb_=sb_: e.matmul(ps[sb_][:, :], lhsT=ones2048[:, :], rhs=sq[q][:, :],
                                                              start=(c == 0), stop=(c == NCH - 1)),
                     [("sq", q), "ones2048"], [("ps", sb_)])
            act(rstd[:, t * TT:(t + 1) * TT], ps[sb_][:, :], AF.Sqrt, [("ps", sb_)], [("rstd", t)], bias=EPS, scale=1.0)
            recip(rstd[:, t * TT:(t + 1) * TT], rstd[:, t * TT:(t + 1) * TT], [("rstd", t)], [("rstd", t)])
            for c in range(NCH):
                r = st["xr"] % 2
                st["xr"] += 1
                tt(xr[r][:, :], xa(c, t), rstd[:, t * TT:(t + 1) * TT], ALU.mult, [xk(c, t), ("rstd", t)], [("xr", r)])
                act(ha(c, t), xr[r][:, :], AF.Identity, [("xr", r), akey, ("modc", l, shp * 16 + c)], [("h", c, t)],
                    scale=a_ap[:, c:c + 1], bias=sh_ap[:, c:c + 1])

    def out_proj(wsrc, blk0, src_fn, src_keys_fn, l, gate_i, after=None):
        ada_need(l, [gate_i])
        for j in range(NCH):
            s = wload(wsrc[blk0 + j])
            for t in range(2):
                b = bank()
                mm(ps[b][:, :], [(wk(s, kc), src_fn(kc, t)) for kc in range(16)],
                   [("W", s)] + src_keys_fn(t), [("ps", b)])
                stt(xa(j, t), ps[b][:, :], mod(l, gate_i)[:, j:j + 1], xa(j, t), ALU.mult, ALU.add,
                    [("ps", b), ("modc", l, gate_i * 16 + j), xk(j, t)], [xk(j, t)])
            if after is not None:
                after(j)

    def zsrc(kc, t):
        return za(kc, t)

    def zkeys(t):
        return [("z", c, t) for c in range(NCH)]

    def hkeys(t):
        return [("h", c, t) for c in range(NCH)]

    def mlp(l, after=None):
        for g in range(4):
            for m in range(16):
                s = wload(w_1[l * 64 + g * 16 + m])
                for t in range(2):
                    b = bank()
                    mm(ps[b][:, :], [(wk(s, kc), ha(kc, t)) for kc in range(16)], [("W", s)] + hkeys(t), [("ps", b)])
                    r = st["xr"] % 2
                    st["xr"] += 1
                    act(xr[r][:, :], ps[b][:, :], AF.Relu, [("ps", b)], [("xr", r)])
                    tt(za(m, t), xr[r][:, :], xr[r][:, :], ALU.mult, [("xr", r)], [("z", m, t)])
            out_proj(w_2, l * 64 + g * 16, zsrc, zkeys, l, 5, after=after if g == 3 else None)

    def conv_mixer(l, jl, half, first):
        halo_col = TH + 1 if half == 0 else 0
        edge_col = TH if half == 0 else 1
        sp_dma(maskt[:, :], masks_d[half, :, :], writes=["maskt"], chname="msk", nch=1)
        if first:
            xc = xcol[:, 16:32] if half == 0 else xcol[:, 0:16]
            act(colsq[:, :], xc, AF.Square, ["xcol"], ["colsq"])
            mm(ps[6][:, 0:1], [(ones2048[:, :], colsq[:, c:c + 1]) for c in range(NCH)], ["colsq", "ones2048"], [("ps", 6)])
            act(colr[:, 0:1], ps[6][:, 0:1], AF.Sqrt, [("ps", 6)], ["colr"], bias=EPS, scale=1.0)
            recip(colr[:, 1:2], colr[:, 0:1], ["colr"], ["colr2"])
            tsc(colt[:, :], xc, colr[:, 1:2], ALU.mult, ["xcol", "colr2"], ["colt"])
            tt(colt[:, :], colt[:, :], a_sc[:, 0:16], ALU.mult, ["colt", "a1"], ["colt"])
            tt(hcol[:, :], colt[:, :], mod(l, 0), ALU.add, ["colt"] + mk(l, 0), ["hcol"])
        def stage_a(c):
            ub = c % 2
            U = ubuf[ub]
            s_cg = wload(w_cin[jl * 48 + c * 3 + 1])
            s_hv = wload(w_cin[jl * 48 + c * 3 + 2])
            for t in range(2):
                b1 = bank(6)
                mm(ps[b1][:, :], [(wk(s_cg, kc), ha(kc, t)) for kc in range(16)], [("W", s_cg)] + hkeys(t), [("ps", b1)])
                b2 = bank(6)
                mm(ps[b2][:, :], [(wk(s_hv, kc), ha(kc, t)) for kc in range(16)], [("W", s_hv)] + hkeys(t), [("ps", b2)])
                r = st["xr"] % 2
                st["xr"] += 1
                act(xr[r][:, :], ps[b1][:, :], AF.Copy, [("ps", b1)], [("xr", r)])
                tt(U[:, 1 + t * TT: 1 + (t + 1) * TT], xr[r][:, :], ps[b2][:, :], ALU.mult,
                   [("xr", r), ("ps", b2)], [("ubuf", ub)])
            if first:
                mm(ps[6][:, 0:1], [(wk(s_cg, kc), hcol[:, kc:kc + 1]) for kc in range(16)], [("W", s_cg), "hcol"], [("ps", 6)])
                mm(ps[6][:, 1:2], [(wk(s_hv, kc), hcol[:, kc:kc + 1]) for kc in range(16)], [("W", s_hv), "hcol"], [("ps", 6)])
                act(colr[:, 2:3], ps[6][:, 0:1], AF.Copy, [("ps", 6)], ["colr3"])
                tt(U[:, halo_col:halo_col + 1], colr[:, 2:3], ps[6][:, 1:2], ALU.mult, ["colr3", ("ps", 6)], [("ubuf", ub)])
                P.op("dve", lambda e, U=U, c=c: e.tensor_copy(out=usave[:, c:c + 1], in_=U[:, edge_col:edge_col + 1]),
                     [("ubuf", ub)], [("usave", c)])
            else:
                P.op("dve", lambda e, U=U, c=c: e.tensor_copy(out=U[:, halo_col:halo_col + 1], in_=usave[:, c:c + 1]),
                     [("usave", c)], [("ubuf", ub)])
            oc = 0 if half == 0 else TH + 1
            P.op("dve", lambda e, U=U: e.memset(U[:, oc:oc + 1], 0.0), [], [("ubuf", ub)])
            A = cacc[ub]
            w0 = convw[:, jl * 48 + 0 * 16 + c: jl * 48 + 0 * 16 + c + 1]
            w1 = convw[:, jl * 48 + 1 * 16 + c: jl * 48 + 1 * 16 + c + 1]
            w2 = convw[:, jl * 48 + 2 * 16 + c: jl * 48 + 2 * 16 + c + 1]
            tt(ct1[:, :], U[:, 0:TH], maskt[:, 0:TH], ALU.mult, [("ubuf", ub), "maskt"], ["ct1"])
            tt(ct2[:, :], U[:, 2:TH + 2], maskt[:, TH:2 * TH], ALU.mult, [("ubuf", ub), "maskt"], ["ct2"])
            tsc(A[:, :], U[:, 1:TH + 1], w1, ALU.mult, [("ubuf", ub), "convw"], [("cacc", ub)])
            stt(A[:, :], ct1[:, :], w0, A[:, :], ALU.mult, ALU.add, ["ct1", "convw", ("cacc", ub)], [("cacc", ub)])
            stt(A[:, :], ct2[:, :], w2, A[:, :], ALU.mult, ALU.add, ["ct2", "convw", ("cacc", ub)], [("cacc", ub)])

        def stage_b(c):
            ub = c % 2
            A = cacc[ub]
            s_bg = wload(w_cin[jl * 48 + c * 3 + 0])
            for t in range(2):
                b = bank(6)
                mm(ps[b][:, :], [(wk(s_bg, kc), ha(kc, t)) for kc in range(16)], [("W", s_bg)] + hkeys(t), [("ps", b)])
                tt(za(c, t), A[:, t * TT:(t + 1) * TT], ps[b][:, :], ALU.mult, [("cacc", ub), ("ps", b)], [("z", c, t)])

        stage_a(0)
        for c in range(NCH):
            if c + 1 < NCH:
                stage_a(c + 1)
            stage_b(c)
        out_proj(w_cout, jl * 16, zsrc, zkeys, l, 2)

    def gmlp_mixer(l):
        for c in range(NCH):
            s = wload(w_gin[c])
            for t in range(2):
                b = bank()
                mm(ps[b][:, :], [(wk(s, kc), ha(kc, t)) for kc in range(16)], [("W", s)] + hkeys(t), [("ps", b)])
                act(za(c, t), ps[b][:, :], AF.Copy, [("ps", b)], [("z", c, t)])
                q = st["sq"] % 2
                st["sq"] += 1
                act(sq[q][:, :], ps[b][:, :], AF.Square, [("ps", b)], [("sq", q)])
                P.op("pe", lambda e, q=q, c=c, t=t: e.matmul(ps[5 + t][:, :], lhsT=ones2048[:, :], rhs=sq[q][:, :],
                                                           start=(c == 0), stop=(c == NCH - 1)),
                     [("sq", q), "ones2048"], [("ps", 5 + t)])
        for t in range(2):
            act(rstd[:, t * TT:(t + 1) * TT], ps[5 + t][:, :], AF.Sqrt, [("ps", 5 + t)], [("rstd", t)], bias=EPS, scale=1.0)
            recip(rstd[:, t * TT:(t + 1) * TT], rstd[:, t * TT:(t + 1) * TT], [("rstd", t)], [("rstd", t)])
        for c in range(NCH):
            for t in range(2):
                stt(za(c, t), za(c, t), gv[:, c:c + 1], rstd[:, t * TT:(t + 1) * TT], ALU.mult, ALU.mult,
                    [("z", c, t), "gv", ("rstd", t)], [("z", c, t)])
        tcount = 0
        for c in range(NCH):
            for t in range(2):
                b = bank()
                for n in range(4):
                    sl = tcount % 8
                    tcount += 1
                    v = vT[sl % 2]
                    zin = z[:, c * TH + t * TT + n * 128: c * TH + t * TT + (n + 1) * 128]
                    P.op("pe", lambda e, sl=sl, zin=zin: e.transpose(out=psb[:, sl * 128:(sl + 1) * 128], in_=zin, identity=ident[:, :]),
                         [("z", c, t), "ident"], [("psb", sl)])
                    P.op("act", lambda e, sl=sl, v=v: e.activation(out=v[:, :], in_=psb[:, sl * 128:(sl + 1) * 128], func=AF.Copy),
                         [("psb", sl)], [("vT", sl % 2)])
                    mm(ps[b][:, n * 128:(n + 1) * 128], [(v[:, :], wsT[:, c * 128:(c + 1) * 128])],
                       [("vT", sl % 2), "wsT"], [("ps", b)])
                for n in range(4):
                    tt(z[:, c * TH + t * TT + n * 128: c * TH + t * TT + (n + 1) * 128], ps[b][:, n * 128:(n + 1) * 128],
                       bsbc[:, c * 128:(c + 1) * 128], ALU.add, [("ps", b), "bsbc"], [("z", c, t)])
        for c in range(NCH):
            s = wload(w_gin[16 + c])
            for t in range(2):
                b = bank()
                mm(ps[b][:, :], [(wk(s, kc), ha(kc, t)) for kc in range(16)], [("W", s)] + hkeys(t), [("ps", b)])
                tt(za(c, t), ps[b][:, :], za(c, t), ALU.mult, [("ps", b), ("z", c, t)], [("z", c, t)])
        out_proj(w_gout, 0, zsrc, zkeys, l, 2)

    def mla_kside(half):
        sp_dma(ropeCk[0:64, :], rope_d[half, :, 0:TH], writes=RCK, chname="rp", nch=2)
        sp_dma(ropeSk[0:64, :], rope_d[half, :, TH:2 * TH], writes=RSK, chname="rp", nch=2)
        for cc in range(4):
            s = wload(w_kva[cc])
            for t in range(2):
                b = bank()
                mm(ps[b][:, :], [(wk(s, kc), ha(kc, t)) for kc in range(16)], [("W", s)] + hkeys(t), [("ps", b)])
                act(zf[:, cc * TH + t * TT: cc * TH + (t + 1) * TT], ps[b][:, :], AF.Copy, [("ps", b)], zfk(cc, t))
                q = st["sq"] % 2
                st["sq"] += 1
                act(sq[q][:, :], ps[b][:, :], AF.Square, [("ps", b)], [("sq", q)])
                P.op("pe", lambda e, q=q, cc=cc, t=t: e.matmul(ps[5 + t][:, :], lhsT=ones512[:, :], rhs=sq[q][:, :],
                                                             start=(cc == 0), stop=(cc == 3)),
                     [("sq", q), "ones512"], [("ps", 5 + t)])
        for t in range(2):
            act(rstd[:, t * TT:(t + 1) * TT], ps[5 + t][:, :], AF.Sqrt, [("ps", 5 + t)], [("rstd", t)], bias=EPS, scale=1.0)
            recip(rstd[:, t * TT:(t + 1) * TT], rstd[:, t * TT:(t + 1) * TT], [("rstd", t)], [("rstd", t)])
        for cc in range(4):
            for t in range(2):
                zz = zf[:, cc * TH + t * TT: cc * TH + (t + 1) * TT]
                stt(zz, zz, gkv[:, cc:cc + 1], rstd[:, t * TT:(t + 1) * TT], ALU.mult, ALU.mult,
                    zfk(cc, t) + ["gkv", ("rstd", t)], zfk(cc, t))
                tok = half * TH + t * TT
                sp_dma(ckvo[:, cc * 2048 + tok: cc * 2048 + tok + TT], zz, reads=zfk(cc, t), chname="ko", nch=2)
                act(ckv_all[:, cc * NKEY + 512 + tok: cc * NKEY + 512 + tok + TT], zz, AF.Copy,
                    zfk(cc, t), [("ckv_all", cc)])
        s = wload(w_kva[4])
        for t in range(2):
            b1 = bank()
            mm(ps[b1][0:64, :], [(wk(s, kc, 0, 64), ha(kc, t)) for kc in range(16)], [("W", s)] + hkeys(t), [("ps", b1)])
            b2 = bank()
            mm(ps[b2][0:64, :], [(wk(s, kc, 64, 128), ha(kc, t)) for kc in range(16)], [("W", s)] + hkeys(t), [("ps", b2)])
            r1 = st["xr"] % 2
            st["xr"] += 1
            r2 = st["xr"] % 2
            st["xr"] += 1
            tt(xr[r1][0:64, :], ps[b1][0:64, :], ropeCk[0:64, t * TT:(t + 1) * TT], ALU.mult, [("ps", b1)] + RCK, [("xr", r1)])
            tt(xr[r2][0:64, :], ps[b2][0:64, :], ropeSk[0:64, t * TT:(t + 1) * TT], ALU.mult, [("ps", b2)] + RSK, [("xr", r2)])
            tt(xr[r1][0:64, :], xr[r1][0:64, :], xr[r2][0:64, :], ALU.add, [("xr", r1), ("xr", r2)], [("xr", r1)])
            tok = half * TH + t * TT
            sp_dma(kpeo[:, tok:tok + TT], xr[r1][0:64, :], reads=[("xr", r1)], chname="ko", nch=2)
            act(kpe_all[0:64, 512 + tok: 512 + tok + TT], xr[r1][0:64, :], AF.Copy, [("xr", r1)], ["kpe_all"])

    def mla_attend(l, half, rope_loaded):
        P.op("pool", lambda e: e.dma_start(out=qpe[0][64:72, :], in_=qmask_d[half, :, 0:TT]), writes=[("qpe", 0)], ch=("w", "q"))
        P.op("pool", lambda e: e.dma_start(out=qpe[1][64:72, :], in_=qmask_d[half, :, TT:2 * TT]), writes=[("qpe", 1)], ch=("w", "q"))
        for cc in range(4):
            s = wload(w_qa[cc])
            for t in range(2):
                b = bank()
                mm(ps[b][:, :], [(wk(s, kc), ha(kc, t)) for kc in range(16)], [("W", s)] + hkeys(t), [("ps", b)])
                act(qan[:, cc * TH + t * TT: cc * TH + (t + 1) * TT], ps[b][:, :], AF.Copy, [("ps", b)], [("qan", cc, t)])
                q = st["sq"] % 2
                st["sq"] += 1
                act(sq[q][:, :], ps[b][:, :], AF.Square, [("ps", b)], [("sq", q)])
                P.op("pe", lambda e, q=q, cc=cc, t=t: e.matmul(ps[5 + t][:, :], lhsT=ones512[:, :], rhs=sq[q][:, :],
                                                             start=(cc == 0), stop=(cc == 3)),
                     [("sq", q), "ones512"], [("ps", 5 + t)])
        for t in range(2):
            act(rstd[:, t * TT:(t + 1) * TT], ps[5 + t][:, :], AF.Sqrt, [("ps", 5 + t)], [("rstd", t)], bias=EPS, scale=1.0)
            recip(rstd[:, t * TT:(t + 1) * TT], rstd[:, t * TT:(t + 1) * TT], [("rstd", t)], [("rstd", t)])
        for cc in range(4):
            for t in range(2):
                qq = qan[:, cc * TH + t * TT: cc * TH + (t + 1) * TT]
                stt(qq, qq, gq[:, cc:cc + 1], rstd[:, t * TT:(t + 1) * TT], ALU.mult, ALU.mult,
                    [("qan", cc, t), "gq", ("rstd", t)], [("qan", cc, t)])
        P.fence(HKEYS, TMPKEYS)
        if not rope_loaded:
            sp_dma(ropeC[0:64, :], rope_d[half, :, 0:TH], writes=["ropeC"], chname="rp", nch=2)
            sp_dma(ropeS[0:64, :], rope_d[half, :, TH:2 * TH], writes=["ropeS"], chname="rp", nch=2)
        qanK = [("qan", cc, t) for cc in range(4) for t in range(2)]
        ckvK = [("ckv_all", cc) for cc in range(4)]
        pcount = 0
        for hd in range(16):
            if hd % 2 == 0:
                s_qb = wload(w_qb[hd // 2])
                s_kvb = wload(w_kvb[hd // 2])
            ho = (hd % 2) * 256
            for k5 in range(5):
                b = 6 + (st["bank"] % 2)
                st["bank"] += 1
                mm(ps[b][:, :], [(wk(s_kvb, kc, ho, ho + 128, KC=4), ckv_all[:, kc * NKEY + k5 * 512: kc * NKEY + (k5 + 1) * 512])
                                 for kc in range(4)], [("W", s_kvb)] + ckvK, [("ps", b)])
                act(knope[:, k5 * 512:(k5 + 1) * 512], ps[b][:, :], AF.Copy, [("ps", b)], ["knope"])
            for k5 in range(5):
                b = 6 + (st["bank"] % 2)
                st["bank"] += 1
                for n in range(4):
                    kt = k5 * 4 + n
                    mm(ps[b][:, n * 128:(n + 1) * 128],
                       [(ckv_all[:, kc * NKEY + kt * 128: kc * NKEY + (kt + 1) * 128], wk(s_kvb, kc, ho + 128, ho + 256, KC=4))
                        for kc in range(4)], [("W", s_kvb)] + ckvK, [("ps", b)])
                P.op("dve", lambda e, b=b, k5=k5: e.tensor_copy(out=vh[:, k5 * 512:(k5 + 1) * 512], in_=ps[b][:, :]),
                     [("ps", b)], ["vh"])
            for qt in range(2):
                b = 6 + (st["bank"] % 2)
                st["bank"] += 1
                qq = st["xr"] % 2
                mm(ps[b][:, :], [(wk(s_qb, kc, ho, ho + 128, KC=4), qan[:, kc * TH + qt * TT: kc * TH + (qt + 1) * TT])
                                 for kc in range(4)], [("W", s_qb)] + qanK, [("ps", b)])
                qb_ = (hd * 2 + qt) % 2
                act(qn[qb_][:, :], ps[b][:, :], AF.Copy, [("ps", b)], [("qn%d" % qb_)])
                b1 = 6 + (st["bank"] % 2)
                st["bank"] += 1
                mm(ps[b1][0:64, :], [(wk(s_qb, kc, ho + 128, ho + 192, KC=4), qan[:, kc * TH + qt * TT: kc * TH + (qt + 1) * TT])
                                     for kc in range(4)], [("W", s_qb)] + qanK, [("ps", b1)])
                b2 = 6 + (st["bank"] % 2)
                st["bank"] += 1
                mm(ps[b2][0:64, :], [(wk(s_qb, kc, ho + 192, ho + 256, KC=4), qan[:, kc * TH + qt * TT: kc * TH + (qt + 1) * TT])
                                     for kc in range(4)], [("W", s_qb)] + qanK, [("ps", b2)])
                r1 = st["xr"] % 2
                st["xr"] += 1
                r2 = st["xr"] % 2
                st["xr"] += 1
                tt(xr[r1][0:64, :], ps[b1][0:64, :], ropeC[0:64, qt * TT:(qt + 1) * TT], ALU.mult, [("ps", b1), "ropeC"], [("xr", r1)])
                tt(xr[r2][0:64, :], ps[b2][0:64, :], ropeS[0:64, qt * TT:(qt + 1) * TT], ALU.mult, [("ps", b2), "ropeS"], [("xr", r2)])
                tt(qpe[qt][0:64, :], xr[r1][0:64, :], xr[r2][0:64, :], ALU.add, [("xr", r1), ("xr", r2)], [("qpe", qt)])
                SB = (0, 1, 2, 5)

                def score(kt):
                    sbk = SB[kt % 4]
                    mm(ps[sbk][:, :], [(knope[:, kt * 128:(kt + 1) * 128], qn[qb_][:, :]),
                                       (kpe_all[0:72, kt * 128:(kt + 1) * 128], qpe[qt][0:72, :])],
                       ["knope", "qn%d" % qb_, "kpe_all", ("qpe", qt)], [("ps", sbk)])
                score(0)
                score(1)
                for kt in range(20):
                    if kt + 2 < 20:
                        score(kt + 2)
                    sbk = SB[kt % 4]
                    pb = pcount % 3
                    pcount += 1
                    act(pT[pb][:, :], ps[sbk][:, :], AF.Exp, [("ps", sbk)], ["pT%d" % pb], scale=SCALE)
                    P.op("pe", lambda e, kt=kt, pb=pb: e.matmul(ps[3][:, :], lhsT=vh[:, kt * 128:(kt + 1) * 128], rhs=pT[pb][:, :],
                                                              start=(kt == 0), stop=(kt == 19)),
                         ["vh", "pT%d" % pb], [("ps", 3)])
                    P.op("pe", lambda e, kt=kt, pb=pb: e.matmul(ps[4][:, :], lhsT=ones1[:, :], rhs=pT[pb][:, :],
                                                              start=(kt == 0), stop=(kt == 19)),
                         ["ones1", "pT%d" % pb], [("ps", 4)])
                recip(rec[:, :], ps[4][:, :], [("ps", 4)], ["rec"])
                tt(za(hd, qt), ps[3][:, :], rec[:, :], ALU.mult, [("ps", 3), "rec"], [("z", hd, qt)])
        out_proj(w_o, 0, zsrc, zkeys, l, 2)
        P.fence(TMPKEYS, HKEYS)

    def final_out(half):
        for t in range(2):
            sb_ = 5 + t
            for c in range(NCH):
                q = st["sq"] % 2
                st["sq"] += 1
                act(sq[q][:, :], xa(c, t), AF.Square, [xk(c, t)], [("sq", q)])
                P.op("pe", lambda e, q=q, c=c, sb_=sb_: e.matmul(ps[sb_][:, :], lhsT=ones2048[:, :], rhs=sq[q][:, :],
                                                              start=(c == 0), stop=(c == NCH - 1)),
                     [("sq", q), "ones2048"], [("ps", sb_)])
            act(rstd[:, t * TT:(t + 1) * TT], ps[sb_][:, :], AF.Sqrt, [("ps", sb_)], [("rstd", t)], bias=EPS, scale=1.0)
            recip(rstd[:, t * TT:(t + 1) * TT], rstd[:, t * TT:(t + 1) * TT], [("rstd", t)], [("rstd", t)])
        for c in range(NCH):
            for t in range(2):
                stt(xa(c, t), xa(c, t), fn[:, c:c + 1], rstd[:, t * TT:(t + 1) * TT], ALU.mult, ALU.mult,
                    [xk(c, t), "fn", ("rstd", t)], [xk(c, t)])
            store_x(half, yT, c)

    orders = [(0, 1), (1, 0), (0, 1), (1, 0)]
    skip_mlp_last = bool(dbg and dbg.get("skip_mlp_last"))
    for l in range(NL):
        kind, jl = l % 3, l // 3
        last_layer = (l == NL - 1)
        layer_scales(l)
        if l == 1:
            P.fence(CONVK, KVK + GMLPK)
            for cc in range(4):
                P.op("pool", lambda e, cc=cc: e.dma_start(out=ckv_all[:, cc * NKEY: cc * NKEY + 512],
                                                         in_=cckv_d[:, cc * 512:(cc + 1) * 512]),
                     writes=[("ckv_all", cc)], ch=("w", "c"))
            P.op("pool", lambda e: e.dma_start(out=kpe_all[0:64, 0:512], in_=ckpe_d[:, :]), writes=["kpe_all"], ch=("w", "c"))
            P.op("pool", lambda e: e.dma_start(out=kpe_all[64:72, :], in_=kmask_d[:, :]), writes=["kpe_all"], ch=("w", "c"))
            sp_dma(bsbc[:, :], bsbc_d[:, :], writes=["bsbc"], chname="msk", nch=1)
            P.op("pool", lambda e: e.dma_start(out=wsT[:, :], in_=wsT_d[:, :]), writes=["wsT"], ch=("w", "i"))
        if l == 2:
            P.fence(GMLPK, MLAK)
            P.fence([("psb", i) for i in range(8)], [("ps", 7)])
        if l == 3:
            P.fence(KVK + MLAK, CONVK)
        for oi, half in enumerate(orders[l]):
            resident = (l > 0 and oi == 0)
            if not resident:
                load_x(half, xT if l == 0 else xs)
            if kind == 2:
                if oi == 1:
                    norm_to_h(l, 1)
                mla_attend(l, half, rope_loaded=False)
            else:
                norm_to_h(l, 1)
                if kind == 0:
                    conv_mixer(l, jl, half, first=(oi == 0))
                else:
                    gmlp_mixer(l)
            will_store = (oi == 0) and not last_layer
            need_xcol = (l + 1 < NL) and ((l + 1) % 3 == 0)
            nxt_mla = (l + 1 < NL) and ((l + 1) % 3 == 2)
            if not (last_layer and skip_mlp_last):
                layer_scales2(l)
                norm_to_h(l, 2)

                def after_chunk(j, half=half, ws=(will_store and not nxt_mla and not need_xcol)):
                    if ws:
                        store_x(half, xs, j)
                mlp(l, after=after_chunk)
            if need_xcol:
                col = TH - 1 if half == 0 else 0
                dst = xcol[:, 0:16] if half == 0 else xcol[:, 16:32]
                P.op("dve", lambda e, col=col, dst=dst: e.tensor_copy(out=dst, in_=x[:, col:NCH * TH:TH]),
                     [xk(c, t) for c in range(NCH) for t in range(2)], ["xcol"])
            if nxt_mla:
                layer_scales(l + 1)
                norm_to_h(l + 1, 1)
                mla_kside(half)
                layer_scales(l)
            if last_layer:
                final_out(half)
            elif will_store and (nxt_mla or need_xcol):
                for j in range(NCH):
                    store_x(half, xs, j)

    P.op("sp", lambda e: None, reads=[], writes=[], ch=None)
    fin = P.ops[-1]
    for i, o in enumerate(P.ops[:-1]):
        if o[3] is not None and o[3][0] in ("yst", "ko"):
            fin[2].add(i)
    P.lower(nc, es)
    es.close()
    return nc, P


def _blk(W, KC=16):
    K, N = W.shape
    assert K == KC * 128
    ncol = 2048 // KC
    nb = N // ncol
    return np.ascontiguousarray(W.reshape(KC, 128, nb, ncol).transpose(2, 1, 0, 3)).reshape(nb, 128, KC * ncol)


def _fm(v):
    v = np.asarray(v, np.float32)
    lead = int(np.prod(v.shape[:-1])) if v.ndim > 1 else 1
    C = v.shape[-1] // 128
    return np.ascontiguousarray(v.reshape(lead, C, 128).transpose(2, 0, 1)).reshape(128, lead * C)


def _swap_cols(W):
    idx = np.concatenate([np.arange(16, 32), np.arange(0, 16), np.arange(48, 64), np.arange(32, 48)])
    return W[..., idx]


def _rope_tables():
    L = 2048
    rows = np.repeat(np.arange(L // 64), 64).astype(np.float32)
    cols = np.tile(np.arange(64), L // 64).astype(np.float32)
    inv = (1.0 / (10000.0 ** (np.arange(16, dtype=np.float32) / 16))).astype(np.float32)
    ar = rows[:, None] * inv
    ac = cols[:, None] * inv
    C = np.concatenate([np.cos(ar), np.cos(ar), np.cos(ac), np.cos(ac)], 1)
    S = np.concatenate([-np.sin(ar), np.sin(ar), -np.sin(ac), np.sin(ac)], 1)
    return C.astype(np.float32), S.astype(np.float32)


def _prep_shared(inp):
    g = {k: np.asarray(v, np.float32) for k, v in inp.items()}
    sh = {}
    sh["adab"] = _fm(g["ada_b"])
    sh["n1"] = _fm(g["norm1"])
    sh["n2"] = _fm(g["norm2"])
    sh["fn"] = _fm(g["final_norm"])
    sh["convw"] = _fm(g["conv_w"])
    sh["gv"] = _fm(g["gmlp_g_v"][0])
    sh["bsbc"] = np.ascontiguousarray(np.broadcast_to(g["gmlp_b_s"][0].reshape(1, 2048), (128, 2048)))
    sh["wsT"] = np.ascontiguousarray(g["gmlp_w_s"][0].transpose(2, 0, 1)).reshape(128, 2048)
    sh["gq"] = _fm(g["mla_g_q"][0])
    sh["gkv"] = _fm(g["mla_g_kv"][0])
    sh["ident"] = np.eye(128, dtype=np.float32)
    sh["w_ada"] = _blk(g["ada_w"].transpose(1, 0, 2).reshape(2048, 4 * 12288)) if False else \
        np.concatenate([_blk(g["ada_w"][l]) for l in range(4)], 0)
    cin = []
    for jl in range(2):
        W = g["conv_w_in"][jl]
        b = _blk(W)
        order = [k * 16 + c for c in range(16) for k in range(3)]
        cin.append(b[order])
    sh["w_cin"] = np.concatenate(cin, 0)
    sh["w_cout"] = np.concatenate([_blk(g["conv_w_out"][jl]) for jl in range(2)], 0)
    gb = _blk(g["gmlp_w_in"][0])
    sh["w_gin"] = np.concatenate([gb[16:32], gb[0:16]], 0)
    sh["w_gout"] = _blk(g["gmlp_w_out"][0])
    sh["w_qa"] = _blk(g["mla_w_q_a"][0])
    kva = g["mla_w_kv_a"][0]
    kva_ext = np.concatenate([kva, _swap_cols(kva[:, 512:576])], 1)
    sh["w_kva"] = _blk(kva_ext)
    qb = g["mla_w_q_b"][0].reshape(512, 16, 192)
    qb_ext = np.concatenate([qb, _swap_cols(qb[:, :, 128:192])], 2).reshape(512, 16 * 256)
    sh["w_qb"] = _blk(qb_ext, KC=4)
    sh["w_kvb"] = _blk(g["mla_w_kv_b"][0], KC=4)
    sh["w_o"] = _blk(g["mla_w_o"][0])
    sh["w_1"] = np.concatenate([_blk(g["mlp_w1"][l]) for l in range(4)], 0)
    w2 = []
    for l in range(4):
        for gq in range(4):
            w2.append(_blk(g["mlp_w2"][l][gq * 2048:(gq + 1) * 2048]))
    sh["w_2"] = np.concatenate(w2, 0)
    return sh


def _prep_core(inp, r):
    g = inp
    pc = {}
    if r < 4:
        X = np.asarray(g["x_prompt"][8 * r: 8 * r + 8], np.float32).reshape(2048, D)
        cond = np.asarray(g["c_ctx"], np.float32)
    else:
        X = np.asarray(g["x_sample"][r - 4], np.float32)
        cond = np.asarray(g["c"][r - 4], np.float32)
    XT = X.reshape(2, TH, NCH, 128).transpose(0, 3, 2, 1)
    pc["xT"] = np.ascontiguousarray(XT).reshape(2, 128, NCH * TH)
    hal = np.stack([X[TH - 1], X[TH]], 0)
    pc["xhalo"] = _fm(hal)
    pc["cond"] = _fm(cond)
    tok = np.arange(2048)
    if r < 4:
        mL = (tok % 256 != 0).astype(np.float32)
        mR = (tok % 256 != 255).astype(np.float32)
    else:
        mL = (tok != 0).astype(np.float32)
        mR = (tok != 2047).astype(np.float32)
    m = np.stack([np.concatenate([mL[hf * TH:(hf + 1) * TH], mR[hf * TH:(hf + 1) * TH]]) for hf in range(2)], 0)
    pc["masks"] = np.ascontiguousarray(np.broadcast_to(m[:, None, :], (2, 128, 2 * TH)))
    if r < 4:
        C = np.ones((2048, 64), np.float32)
        S = np.zeros((2048, 64), np.float32)
    else:
        C, S = _rope_tables()
    pc["rope"] = np.ascontiguousarray(np.stack(
        [np.concatenate([C[hf * TH:(hf + 1) * TH].T, S[hf * TH:(hf + 1) * TH].T], 1) for hf in range(2)], 0))
    km = np.zeros((8, NKEY), np.float32)
    qm = np.zeros((2, 8, TH), np.float32)
    if r < 4:
        seq = tok // 256
        km[:, :] = NEG
        for s_ in range(8):
            km[s_, 512 + np.nonzero(seq == s_)[0]] = 0.0
        for hf in range(2):
            for t in range(TH):
                qm[hf, seq[hf * TH + t], t] = 1.0
        cck = np.zeros((128, 4 * 512), np.float32)
        ckp = np.zeros((64, 512), np.float32)
    else:
        qm[:, 0, :] = 1.0
        cc = np.asarray(g["cache_ckv"][r - 4, 0], np.float32)
        cck = np.ascontiguousarray(cc.reshape(512, 4, 128).transpose(2, 1, 0)).reshape(128, 4 * 512)
        ckp = np.ascontiguousarray(np.asarray(g["cache_kpe"][r - 4, 0], np.float32).T)
    pc["kmask"] = km
    pc["qmask"] = qm
    pc["cckv"] = cck
    pc["ckpe"] = ckp
    return pc


_CACHE = {}


def kernel(**inputs):
    sh = _prep_shared(inputs)
    in_maps = []
    for r in range(8):
        m = dict(sh)
        m.update(_prep_core(inputs, r))
        in_maps.append(m)
    if "nc" not in _CACHE:
        _CACHE["nc"] = build_program()[0]
    nc = _CACHE["nc"]
    res = run_bass_kernel_spmd(nc, in_maps, core_ids=list(range(8)))
    outs = res.results
    y = []
    for r in range(8):
        yT = np.asarray(outs[r]["yT"]).reshape(2, 128, NCH, TH)
        y.append(np.ascontiguousarray(yT.transpose(0, 3, 2, 1)).reshape(2048, D))
    y_prompt = np.stack(y[0:4], 0).reshape(32, 256, D).astype(np.float32)
    y_sample = np.stack(y[4:8], 0).reshape(4, 2048, D).astype(np.float32)
    ck = []
    kp = []
    for r in range(4):
        c = np.asarray(outs[r]["ckvo"]).reshape(128, 4, 2048)
        ck.append(np.ascontiguousarray(c.transpose(2, 1, 0)).reshape(2048, 512))
        kp.append(np.ascontiguousarray(np.asarray(outs[r]["kpeo"]).T))
    new_ckv = np.stack(ck, 0).reshape(32, 256, 512)[:, None].astype(np.float32)
    new_kpe = np.stack(kp, 0).reshape(32, 256, 64)[:, None].astype(np.float32)
    return (y_prompt, y_sample, np.ascontiguousarray(new_ckv), np.ascontiguousarray(new_kpe))
```

```python
import numpy as np
import concourse.bass as bass
import concourse.mybir as mybir
from concourse.bass_utils import run_bass_kernel_spmd

F32 = mybir.dt.float32
BF16 = mybir.dt.bfloat16
ALU = mybir.AluOpType
AF = mybir.ActivationFunctionType

D = 2048
NCH = 16
TH = 1024
TT = 512
NKEY = 2560
NSLOT = 5
EPS = 1e-6
SCALE = float(192 ** -0.5)
NEG = -30000.0


class Prog:
    def __init__(self):
        self.ops = []
        self.last_w = {}
        self.readers = {}
        self.joins = {}
        self.ch_last = {}

    def _exp(self, node):
        if node >= 0:
            return (node,)
        return self.joins[node]

    def op(self, eng, emit, reads=(), writes=(), ch=None):
        deps = set()
        for k in reads:
            w = self.last_w.get(k)
            if w is not None:
                deps.update(self._exp(w))
        for k in writes:
            w = self.last_w.get(k)
            if w is not None:
                deps.update(self._exp(w))
            for r in self.readers.get(k, ()):
                deps.update(self._exp(r))
        if ch is not None and ch in self.ch_last:
            deps.add(self.ch_last[ch])
        oid = len(self.ops)
        self.ops.append([eng, emit, deps, ch])
        for k in reads:
            self.readers.setdefault(k, []).append(oid)
        for k in writes:
            self.last_w[k] = oid
            self.readers[k] = []
        if ch is not None:
            self.ch_last[ch] = oid
        return oid

    def fence(self, from_keys, to_keys):
        s = set()
        for k in from_keys:
            w = self.last_w.get(k)
            if w is not None:
                s.update(self._exp(w))
            for r in self.readers.get(k, ()):
                s.update(self._exp(r))
        jid = -(len(self.joins) + 1)
        self.joins[jid] = tuple(s)
        for k in to_keys:
            self.last_w[k] = jid
            self.readers[k] = []

    def lower(self, nc, es):
        ops = self.ops
        n = len(ops)
        for o in ops:
            if o[0] == "pe":
                o[2] = {d for d in o[2] if ops[d][0] != "pe"}
        needed = [False] * n
        for o in ops:
            for d in o[2]:
                needed[d] = True
        sems = {}

        def sem(name):
            if name not in sems:
                sems[name] = es.enter_context(nc.semaphore("s_" + str(len(sems))))
            return sems[name]

        sig = [None] * n
        tick = {}
        chcnt = {}
        for i, o in enumerate(ops):
            if o[3] is not None:
                c = chcnt.get(o[3], 0) + 1
                chcnt[o[3]] = c
                sig[i] = (("ch",) + tuple(o[3]) if isinstance(o[3], tuple) else ("ch", o[3]), 16 * c)
            elif needed[i]:
                t = tick.get(o[0], 0) + 1
                tick[o[0]] = t
                sig[i] = (("eng", o[0]), t)
        per_eng = {"pe": [], "act": [], "dve": [], "pool": [], "sp": []}
        for i, o in enumerate(ops):
            per_eng[o[0]].append(i)
        self.stats = {k: len(v) for k, v in per_eng.items()}
        self.stats["ticks"] = dict(tick)
        self.stats["ch"] = len(chcnt)

        def run(eng_name, e):
            waited = {}
            for i in per_eng[eng_name]:
                o = ops[i]
                need = {}
                for d in o[2]:
                    s, v = sig[d]
                    if v > need.get(s, 0):
                        need[s] = v
                for s, v in need.items():
                    if v > waited.get(s, 0):
                        e.wait_ge(sem(s), v)
                        waited[s] = v
                ins = o[1](e)
                if sig[i] is not None and ins is not None:
                    s, v = sig[i]
                    ins.then_inc(sem(s), 16 if s[0] == "ch" else 1)

        for i in range(n):
            if sig[i] is not None:
                sem(sig[i][0])
        block = es.enter_context(nc.Block())

        @block.tensor
        def _(e):
            run("pe", e)

        @block.scalar
        def _(e):
            run("act", e)

        @block.vector
        def _(e):
            run("dve", e)

        @block.gpsimd
        def _(e):
            run("pool", e)

        @block.sync
        def _(e):
            run("sp", e)


def build_program(dbg=None):
    from contextlib import ExitStack
    nc = bass.Bass("TRN2", target_bir_lowering=False)
    P = Prog()
    es = ExitStack()

    def din(name, shape):
        return nc.dram_tensor(name, list(shape), F32, kind="ExternalInput").ap()

    def dout(name, shape):
        return nc.dram_tensor(name, list(shape), F32, kind="ExternalOutput").ap()

    xT = din("xT", [2, 128, NCH * TH])
    xhalo = din("xhalo", [128, 32])
    cond_d = din("cond", [128, 16])
    masks_d = din("masks", [2, 128, 2 * TH])
    rope_d = din("rope", [2, 64, 2 * TH])
    kmask_d = din("kmask", [8, NKEY])
    qmask_d = din("qmask", [2, 8, TH])
    cckv_d = din("cckv", [128, 4 * 512])
    ckpe_d = din("ckpe", [64, 512])
    adab_d = din("adab", [128, 4 * 96])
    n1_d = din("n1", [128, 64])
    n2_d = din("n2", [128, 64])
    fn_d = din("fn", [128, 16])
    convw_d = din("convw", [128, 96])
    gv_d = din("gv", [128, 16])
    bsbc_d = din("bsbc", [128, 2048])
    wsT_d = din("wsT", [128, 2048])
    gq_d = din("gq", [128, 4])
    gkv_d = din("gkv", [128, 4])
    ident_d = din("ident", [128, 128])
    w_ada = din("w_ada", [384, 128, 2048])
    w_cin = din("w_cin", [96, 128, 2048])
    w_cout = din("w_cout", [32, 128, 2048])
    w_gin = din("w_gin", [32, 128, 2048])
    w_gout = din("w_gout", [16, 128, 2048])
    w_qa = din("w_qa", [4, 128, 2048])
    w_kva = din("w_kva", [5, 128, 2048])
    w_qb = din("w_qb", [8, 128, 2048])
    w_kvb = din("w_kvb", [8, 128, 2048])
    w_o = din("w_o", [16, 128, 2048])
    w_1 = din("w_1", [256, 128, 2048])
    w_2 = din("w_2", [256, 128, 2048])
    yT = dout("yT", [2, 128, NCH * TH])
    ckvo = dout("ckvo", [128, 4 * 2048])
    kpeo = dout("kpeo", [64, 2048])
    xs = nc.dram_tensor("xs_scratch", [2, 128, NCH * TH], F32).ap()

    off = [20480]

    def sb(name, cols, dt, at=None):
        nbytes = cols * (4 if dt == F32 else 2)
        if at is None:
            o = off[0]
            off[0] = (o + nbytes + 63) // 64 * 64
        else:
            o = at
        t = nc.alloc_sbuf_tensor_at(name, [128, cols], dt, offset=o)
        return t, o

    x, _ = sb("x", NCH * TH, F32)
    h, h_off = sb("h", NCH * TH, BF16)
    z, z_off = sb("z", NCH * TH, BF16)
    zf, _ = sb("zf", NCH * TH // 2, F32, at=z_off)
    wbuf = [sb("w%d" % i, 2048, BF16)[0] for i in range(NSLOT)]
    mods, _ = sb("mods", 384, F32)
    adab, _ = sb("adab", 384, F32)
    n1, _ = sb("n1", 64, F32)
    n2, _ = sb("n2", 64, F32)
    fn, _ = sb("fn", 16, F32)
    a_sc, _ = sb("a_sc", 32, F32)
    condt, _ = sb("condt", 16, F32)
    sfm, _ = sb("sfm", 16, BF16)
    ones2048, _ = sb("ones2048", 128, BF16)
    ones512, _ = sb("ones512", 128, BF16)
    ones1, _ = sb("ones1", 128, BF16)
    ident, _ = sb("ident", 128, BF16)
    convw, _ = sb("convw", 96, F32)
    gv, _ = sb("gv", 16, F32)
    gq, _ = sb("gq", 4, F32)
    gkv, _ = sb("gkv", 4, F32)
    xcol, _ = sb("xcol", 32, F32)
    hcol, _ = sb("hcol", 16, BF16)
    usave, _ = sb("usave", 16, F32)
    colt, _ = sb("colt", 16, F32)
    colsq, _ = sb("colsq", 16, BF16)
    colr, _ = sb("colr", 4, F32)
    rstd, _ = sb("rstd", TH, F32)
    sq = [sb("sq%d" % i, TT, BF16)[0] for i in range(2)]
    xr = [sb("xr%d" % i, TT, F32)[0] for i in range(2)]
    arena = off[0]
    off[0] = arena
    ubuf = [sb("ubuf%d" % i, TH + 2, F32)[0] for i in range(2)]
    maskt, _ = sb("maskt", 2 * TH, F32)
    ct1, _ = sb("ct1", TH, F32)
    ct2, _ = sb("ct2", TH, F32)
    cacc = [sb("cacc%d" % i, TH, F32)[0] for i in range(2)]
    end_conv = off[0]
    off[0] = arena
    ckv_all, _ = sb("ckv_all", 4 * NKEY, BF16)
    kpe_all, _ = sb("kpe_all", NKEY, BF16)
    xreg = off[0]
    bsbc, _ = sb("bsbc", 2048, F32)
    wsT, _ = sb("wsT", 2048, BF16)
    vT = [sb("vT%d" % i, 128, BF16)[0] for i in range(2)]
    end_gmlp = off[0]
    off[0] = xreg
    qan, _ = sb("qan", 4 * TH, BF16)
    rec, _ = sb("rec", TT, F32)
    qpe = [sb("qpe%d" % i, TT, BF16)[0] for i in range(2)]
    end_mla = off[0]
    assert max(end_conv, end_gmlp, end_mla) <= 229376, (end_conv, end_gmlp, end_mla)
    ropeCk, _ = sb("ropeCk", TH, F32, at=z_off + 16384)
    ropeSk, _ = sb("ropeSk", TH, F32, at=z_off + 20480)
    knope, _ = sb("knope", NKEY, BF16, at=h_off)
    vh, _ = sb("vh", NKEY, BF16, at=h_off + 3 * 2048)
    pT = [sb("pT%d" % i, TT, BF16, at=h_off + 6 * 2048 + i * 1024)[0] for i in range(3)]
    qn = [sb("qn%d" % i, TT, BF16, at=h_off + 7 * 2048 + 1024 + i * 1024)[0] for i in range(2)]
    ropeC, _ = sb("ropeC", TH, F32, at=h_off + 10 * 2048)
    ropeS, _ = sb("ropeS", TH, F32, at=h_off + 12 * 2048)
    ps = [nc.alloc_psum_tensor("ps%d" % i, [128, 512], F32) for i in range(8)]
    psb = ps[7].bitcast(BF16)

    HKEYS = [("h", c, t) for c in range(NCH) for t in range(2)]
    TMPKEYS = ["knope", "vh", "pT0", "pT1", "pT2", "qn0", "qn1", "ropeC", "ropeS"]
    CONVK = [("ubuf", 0), ("ubuf", 1), "maskt", "ct1", "ct2", ("cacc", 0), ("cacc", 1)]
    KVK = [("ckv_all", cc) for cc in range(4)] + ["kpe_all"]
    GMLPK = ["bsbc", "wsT", ("vT", 0), ("vT", 1)]
    MLAK = [("qan", cc, t) for cc in range(4) for t in range(2)] + ["rec", ("qpe", 0), ("qpe", 1)]
    RCK = [("z", c, t) for c in (8, 9) for t in range(2)]
    RSK = [("z", c, t) for c in (10, 11) for t in range(2)]

    def zfk(cc, t):
        return [("z", 2 * cc + t, 0), ("z", 2 * cc + t, 1)]

    st = {"slot": 0, "bank": 0, "spch": 0, "xr": 0, "sq": 0}

    ada_q = []
    st["wl"] = 0
    st["inpump"] = False
    ADA_EVERY = 2

    def wload(src_ap):
        if not st["inpump"]:
            st["wl"] += 1
            if st["wl"] % ADA_EVERY == 0:
                ada_pump(1)
        s = st["slot"]
        st["slot"] = (s + 1) % NSLOT
        P.op("pool", lambda e, s=s, src_ap=src_ap: e.dma_start(out=wbuf[s][:, :], in_=src_ap),
             writes=[("W", s)], ch=("w", s))
        return s

    def ada_pump(n):
        st["inpump"] = True
        while n > 0 and ada_q:
            l, j = ada_q.pop(0)
            s = wload(w_ada[l * 96 + j])
            b = bank()
            mm(ps[b][:, 0:1], [(wk(s, kc), sfm[:, kc:kc + 1]) for kc in range(16)], [("W", s), "sfm"], [("ps", b)])
            tt(mods[:, l * 96 + j: l * 96 + j + 1], ps[b][:, 0:1], adab[:, l * 96 + j: l * 96 + j + 1], ALU.add,
               [("ps", b), "adab"], [("modc", l, j)])
            n -= 1
        st["inpump"] = False

    def ada_need(l, parts):
        last = (l, max(parts) * 16 + 15)
        while ada_q and (ada_q[0][0], ada_q[0][1]) <= last:
            ada_pump(1)

    def mk(l, i):
        return [("modc", l, i * 16 + c) for c in range(16)]

    def bank(nb=5):
        b = st["bank"] % nb
        st["bank"] += 1
        return b

    def sp_dma(out_ap, in_ap, reads=(), writes=(), eng="sp", nch=4, chname="sp"):
        c = st["spch"] % nch
        st["spch"] += 1
        return P.op(eng, lambda e: e.dma_start(out=out_ap, in_=in_ap), reads=reads, writes=writes,
                    ch=(chname, c))

    def mm(out_ap, pairs, reads, writes):
        def emit(e):
            ins = None
            n = len(pairs)
            for i, (l, r) in enumerate(pairs):
                ins = e.matmul(out_ap, lhsT=l, rhs=r, start=(i == 0), stop=(i == n - 1))
            return ins
        return P.op("pe", emit, reads, writes)

    def act(out_ap, in_ap, func, reads, writes, **kw):
        return P.op("act", lambda e: e.activation(out=out_ap, in_=in_ap, func=func, **kw), reads, writes)

    def tt(out_ap, in0, in1, op, reads, writes, eng="dve"):
        return P.op(eng, lambda e: e.tensor_tensor(out=out_ap, in0=in0, in1=in1, op=op), reads, writes)

    def tsc(out_ap, in0, s1, op0, reads, writes, s2=None, op1=None, eng="dve"):
        if op1 is None:
            return P.op(eng, lambda e: e.tensor_scalar(out=out_ap, in0=in0, scalar1=s1, scalar2=None, op0=op0),
                        reads, writes)
        return P.op(eng, lambda e: e.tensor_scalar(out=out_ap, in0=in0, scalar1=s1, scalar2=s2, op0=op0, op1=op1),
                    reads, writes)

    def stt(out_ap, in0, scalar, in1, op0, op1, reads, writes, eng="dve"):
        return P.op(eng, lambda e: e.scalar_tensor_tensor(out=out_ap, in0=in0, scalar=scalar, in1=in1,
                                                          op0=op0, op1=op1), reads, writes)

    def recip(out_ap, in_ap, reads, writes):
        return P.op("dve", lambda e: e.reciprocal(out=out_ap, in_=in_ap), reads, writes)

    def xk(c, t):
        return ("x", c, t)

    def xa(c, t):
        return x[:, c * TH + t * TT: c * TH + (t + 1) * TT]

    def ha(c, t):
        return h[:, c * TH + t * TT: c * TH + (t + 1) * TT]

    def za(c, t):
        return z[:, c * TH + t * TT: c * TH + (t + 1) * TT]

    def wk(s, kc, lo=0, hi=128, KC=16):
        w = 2048 // KC
        return wbuf[s][:, kc * w + lo: kc * w + hi]

    for dst, src, key in [(condt, cond_d, "cond"), (adab, adab_d, "adab"), (n1, n1_d, "n1"), (n2, n2_d, "n2"),
                          (fn, fn_d, "fn"), (convw, convw_d, "convw"), (gv, gv_d, "gv"), (gq, gq_d, "gq"),
                          (gkv, gkv_d, "gkv"), (xcol, xhalo, "xcol")]:
        sp_dma(dst[:, :], src[:, :], writes=[key])
    P.op("pool", lambda e: e.dma_start(out=ident[:, :], in_=ident_d[:, :]), writes=["ident"], ch=("w", "i"))
    P.op("dve", lambda e: e.memset(ones2048[:, :], 1.0 / 2048.0), writes=["ones2048"])
    P.op("dve", lambda e: e.memset(ones512[:, :], 1.0 / 512.0), writes=["ones512"])
    P.op("dve", lambda e: e.memset(ones1[:, :], 1.0), writes=["ones1"])
    for i in range(2):
        P.op("dve", lambda e, i=i: e.memset(ubuf[i][:, :], 0.0), writes=[("ubuf", i)])

    act(sfm[:, :], condt[:, :], AF.Silu, ["cond"], ["sfm"])
    NL = 4 if dbg is None else dbg.get("nl", 4)
    for l in range(NL):
        for j in range(96):
            ada_q.append((l, j))
    ada_pump(48)

    def mod(l, i):
        return mods[:, l * 96 + i * 16: l * 96 + (i + 1) * 16]

    def layer_scales(l):
        ada_need(l, [1])
        stt(a_sc[:, 0:16], mod(l, 1), 1.0, n1[:, l * 16:(l + 1) * 16], ALU.add, ALU.mult,
            mk(l, 1) + ["n1"], ["a1"])

    def layer_scales2(l):
        ada_need(l, [4])
        stt(a_sc[:, 16:32], mod(l, 4), 1.0, n2[:, l * 16:(l + 1) * 16], ALU.add, ALU.mult,
            mk(l, 4) + ["n2"], ["a2"])

    def load_x(half, src):
        for c in range(NCH):
            rd = [("xs", half, c)] if src is xs else []
            sp_dma(x[:, c * TH:(c + 1) * TH], src[half, :, c * TH:(c + 1) * TH], reads=rd,
                   writes=[xk(c, 0), xk(c, 1)], chname="xl")

    def store_x(half, dst, c):
        wr = [("xs", half, c)] if dst is xs else []
        sp_dma(dst[half, :, c * TH:(c + 1) * TH], x[:, c * TH:(c + 1) * TH], reads=[xk(c, 0), xk(c, 1)],
               writes=wr, chname="xst" if dst is xs else "yst")

    def norm_gen(l, which, t):
        a_ap = a_sc[:, 0:16] if which == 1 else a_sc[:, 16:32]
        sh_ap = mod(l, 0) if which == 1 else mod(l, 3)
        akey = "a1" if which == 1 else "a2"
        shp = 0 if which == 1 else 3
        ada_need(l, [shp])
        sb_ = 5 + t
        for c in range(NCH):
            q = st["sq"] % 2
            st["sq"] += 1
            act(sq[q][:, :], xa(c, t), AF.Square, [xk(c, t)], [("sq", q)])
            P.op("pe", lambda e, q=q, c=c, sb_=sb_: e.matmul(ps[sb_][:, :], lhsT=ones2048[:, :], rhs=sq[q][:, :],
                                                          start=(c == 0), stop=(c == NCH - 1)),
                 [("sq", q), "ones2048"], [("ps", sb_)])
            yield
        act(rstd[:, t * TT:(t + 1) * TT], ps[sb_][:, :], AF.Sqrt, [("ps", sb_)], [("rstd", t)], bias=EPS, scale=1.0)
        recip(rstd[:, t * TT:(t + 1) * TT], rstd[:, t * TT:(t + 1) * TT], [("rstd", t)], [("rstd", t)])
        yield
        for c in range(NCH):
            r = st["xr"] % 2
            st["xr"] += 1
            tt(xr[r][:, :], xa(c, t), rstd[:, t * TT:(t + 1) * TT], ALU.mult, [xk(c, t), ("rstd", t)], [("xr", r)])
            act(ha(c, t), xr[r][:, :], AF.Identity, [("xr", r), akey, ("modc", l, shp * 16 + c)], [("h", c, t)],
                scale=a_ap[:, c:c + 1], bias=sh_ap[:, c:c + 1])
            yield

    def run_gen(g, n=None):
        if g is None:
            return
        k = 0
        for _ in g:
            k += 1
            if n is not None and k >= n:
                return

    def norm_to_h(l, which):
        run_gen(norm_gen(l, which, 0))
        run_gen(norm_gen(l, which, 1))

    def out_proj(wsrc, blk0, src_fn, src_keys_fn, l, gate_i, after=None, tgen=None, t_outer=False):
        ada_need(l, [gate_i])

        def one(j, t, s):
            b = bank()
            mm(ps[b][:, :], [(wk(s, kc), src_fn(kc, t)) for kc in range(16)],
               [("W", s)] + src_keys_fn(t), [("ps", b)])
            stt(xa(j, t), ps[b][:, :], mod(l, gate_i)[:, j:j + 1], xa(j, t), ALU.mult, ALU.add,
                [("ps", b), ("modc", l, gate_i * 16 + j), xk(j, t)], [xk(j, t)])

        if not t_outer:
            for j in range(NCH):
                s = wload(wsrc[blk0 + j])
                for t in range(2):
                    one(j, t, s)
                if after is not None:
                    after(j)
            return
        for j in range(NCH):
            one(j, 0, wload(wsrc[blk0 + j]))
        for j in range(NCH):
            one(j, 1, wload(wsrc[blk0 + j]))
            run_gen(tgen, 3)
            if after is not None:
                after(j)
        run_gen(tgen)

    def zsrc(kc, t):
        return za(kc, t)

    def zkeys(t):
        return [("z", c, t) for c in range(NCH)]

    def hkeys(t):
        return [("h", c, t) for c in range(NCH)]

    def mlp(l, after=None, tgen=None):
        for g in range(4):
            def w1blk(m, t, s):
                b = bank()
                mm(ps[b][:, :], [(wk(s, kc), ha(kc, t)) for kc in range(16)], [("W", s)] + hkeys(t), [("ps", b)])
                r = st["xr"] % 2
                st["xr"] += 1
                act(xr[r][:, :], ps[b][:, :], AF.Relu, [("ps", b)], [("xr", r)])
                tt(za(m, t), xr[r][:, :], xr[r][:, :], ALU.mult, [("xr", r)], [("z", m, t)])
            m0 = 0
            if g == 0:
                HB = 4
                st["inpump"] = True
                ss = [wload(w_1[l * 64 + m]) for m in range(HB)]
                st["inpump"] = False
                for m in range(HB):
                    w1blk(m, 0, ss[m])
                for m in range(HB):
                    w1blk(m, 1, ss[m])
                m0 = HB
            for m in range(m0, 16):
                s = wload(w_1[l * 64 + g * 16 + m])
                for t in range(2):
                    w1blk(m, t, s)
            if g == 3:
                out_proj(w_2, l * 64 + g * 16, zsrc, zkeys, l, 5, after=after, tgen=tgen, t_outer=True)
            else:
                out_proj(w_2, l * 64 + g * 16, zsrc, zkeys, l, 5)

    def conv_mixer(l, jl, half, first, tgen=None):
        halo_col = TH + 1 if half == 0 else 0
        edge_col = TH if half == 0 else 1
        sp_dma(maskt[:, :], masks_d[half, :, :], writes=["maskt"], chname="msk", nch=1)
        if first:
            xc = xcol[:, 16:32] if half == 0 else xcol[:, 0:16]
            act(colsq[:, :], xc, AF.Square, ["xcol"], ["colsq"])
            mm(ps[6][:, 0:1], [(ones2048[:, :], colsq[:, c:c + 1]) for c in range(NCH)], ["colsq", "ones2048"], [("ps", 6)])
            act(colr[:, 0:1], ps[6][:, 0:1], AF.Sqrt, [("ps", 6)], ["colr"], bias=EPS, scale=1.0)
            recip(colr[:, 1:2], colr[:, 0:1], ["colr"], ["colr2"])
            tsc(colt[:, :], xc, colr[:, 1:2], ALU.mult, ["xcol", "colr2"], ["colt"])
            tt(colt[:, :], colt[:, :], a_sc[:, 0:16], ALU.mult, ["colt", "a1"], ["colt"])
            tt(hcol[:, :], colt[:, :], mod(l, 0), ALU.add, ["colt"] + mk(l, 0), ["hcol"])
        def stage_a(c):
            ub = c % 2
            U = ubuf[ub]
            s_cg = wload(w_cin[jl * 48 + c * 3 + 1])
            s_hv = wload(w_cin[jl * 48 + c * 3 + 2])
            for t in range(2):
                b1 = bank(6)
                mm(ps[b1][:, :], [(wk(s_cg, kc), ha(kc, t)) for kc in range(16)], [("W", s_cg)] + hkeys(t), [("ps", b1)])
                b2 = bank(6)
                mm(ps[b2][:, :], [(wk(s_hv, kc), ha(kc, t)) for kc in range(16)], [("W", s_hv)] + hkeys(t), [("ps", b2)])
                r = st["xr"] % 2
                st["xr"] += 1
                act(xr[r][:, :], ps[b1][:, :], AF.Copy, [("ps", b1)], [("xr", r)])
                tt(U[:, 1 + t * TT: 1 + (t + 1) * TT], xr[r][:, :], ps[b2][:, :], ALU.mult,
                   [("xr", r), ("ps", b2)], [("ubuf", ub)])
            if first:
                mm(ps[6][:, 0:1], [(wk(s_cg, kc), hcol[:, kc:kc + 1]) for kc in range(16)], [("W", s_cg), "hcol"], [("ps", 6)])
                mm(ps[6][:, 1:2], [(wk(s_hv, kc), hcol[:, kc:kc + 1]) for kc in range(16)], [("W", s_hv), "hcol"], [("ps", 6)])
                act(colr[:, 2:3], ps[6][:, 0:1], AF.Copy, [("ps", 6)], ["colr3"])
                tt(U[:, halo_col:halo_col + 1], colr[:, 2:3], ps[6][:, 1:2], ALU.mult, ["colr3", ("ps", 6)], [("ubuf", ub)])
                P.op("dve", lambda e, U=U, c=c: e.tensor_copy(out=usave[:, c:c + 1], in_=U[:, edge_col:edge_col + 1]),
                     [("ubuf", ub)], [("usave", c)])
            else:
                P.op("dve", lambda e, U=U, c=c: e.tensor_copy(out=U[:, halo_col:halo_col + 1], in_=usave[:, c:c + 1]),
                     [("usave", c)], [("ubuf", ub)])
            oc = 0 if half == 0 else TH + 1
            P.op("dve", lambda e, U=U: e.memset(U[:, oc:oc + 1], 0.0), [], [("ubuf", ub)])
            A = cacc[ub]
            w0 = convw[:, jl * 48 + 0 * 16 + c: jl * 48 + 0 * 16 + c + 1]
            w1 = convw[:, jl * 48 + 1 * 16 + c: jl * 48 + 1 * 16 + c + 1]
            w2 = convw[:, jl * 48 + 2 * 16 + c: jl * 48 + 2 * 16 + c + 1]
            tt(ct1[:, :], U[:, 0:TH], maskt[:, 0:TH], ALU.mult, [("ubuf", ub), "maskt"], ["ct1"])
            tt(ct2[:, :], U[:, 2:TH + 2], maskt[:, TH:2 * TH], ALU.mult, [("ubuf", ub), "maskt"], ["ct2"])
            tsc(A[:, :], U[:, 1:TH + 1], w1, ALU.mult, [("ubuf", ub), "convw"], [("cacc", ub)])
            stt(A[:, :], ct1[:, :], w0, A[:, :], ALU.mult, ALU.add, ["ct1", "convw", ("cacc", ub)], [("cacc", ub)])
            stt(A[:, :], ct2[:, :], w2, A[:, :], ALU.mult, ALU.add, ["ct2", "convw", ("cacc", ub)], [("cacc", ub)])

        def stage_b(c):
            ub = c % 2
            A = cacc[ub]
            s_bg = wload(w_cin[jl * 48 + c * 3 + 0])
            for t in range(2):
                b = bank(6)
                mm(ps[b][:, :], [(wk(s_bg, kc), ha(kc, t)) for kc in range(16)], [("W", s_bg)] + hkeys(t), [("ps", b)])
                tt(za(c, t), A[:, t * TT:(t + 1) * TT], ps[b][:, :], ALU.mult, [("cacc", ub), ("ps", b)], [("z", c, t)])

        stage_a(0)
        for c in range(NCH):
            if c + 1 < NCH:
                stage_a(c + 1)
            stage_b(c)
        out_proj(w_cout, jl * 16, zsrc, zkeys, l, 2, tgen=tgen, t_outer=True)

    def gmlp_mixer(l, tgen=None):
        for c in range(NCH):
            s = wload(w_gin[c])
            for t in range(2):
                b = bank()
                mm(ps[b][:, :], [(wk(s, kc), ha(kc, t)) for kc in range(16)], [("W", s)] + hkeys(t), [("ps", b)])
                act(za(c, t), ps[b][:, :], AF.Copy, [("ps", b)], [("z", c, t)])
                q = st["sq"] % 2
                st["sq"] += 1
                act(sq[q][:, :], ps[b][:, :], AF.Square, [("ps", b)], [("sq", q)])
                P.op("pe", lambda e, q=q, c=c, t=t: e.matmul(ps[5 + t][:, :], lhsT=ones2048[:, :], rhs=sq[q][:, :],
                                                           start=(c == 0), stop=(c == NCH - 1)),
                     [("sq", q), "ones2048"], [("ps", 5 + t)])
        for t in range(2):
            act(rstd[:, t * TT:(t + 1) * TT], ps[5 + t][:, :], AF.Sqrt, [("ps", 5 + t)], [("rstd", t)], bias=EPS, scale=1.0)
            recip(rstd[:, t * TT:(t + 1) * TT], rstd[:, t * TT:(t + 1) * TT], [("rstd", t)], [("rstd", t)])
        for c in range(NCH):
            for t in range(2):
                stt(za(c, t), za(c, t), gv[:, c:c + 1], rstd[:, t * TT:(t + 1) * TT], ALU.mult, ALU.mult,
                    [("z", c, t), "gv", ("rstd", t)], [("z", c, t)])
        tcount = 0
        for c in range(NCH):
            for t in range(2):
                b = bank()
                for n in range(4):
                    sl = tcount % 8
                    tcount += 1
                    v = vT[sl % 2]
                    zin = z[:, c * TH + t * TT + n * 128: c * TH + t * TT + (n + 1) * 128]
                    P.op("pe", lambda e, sl=sl, zin=zin: e.transpose(out=psb[:, sl * 128:(sl + 1) * 128], in_=zin, identity=ident[:, :]),
                         [("z", c, t), "ident"], [("psb", sl)])
                    P.op("act", lambda e, sl=sl, v=v: e.activation(out=v[:, :], in_=psb[:, sl * 128:(sl + 1) * 128], func=AF.Copy),
                         [("psb", sl)], [("vT", sl % 2)])
                    mm(ps[b][:, n * 128:(n + 1) * 128], [(v[:, :], wsT[:, c * 128:(c + 1) * 128])],
                       [("vT", sl % 2), "wsT"], [("ps", b)])
                for n in range(4):
                    tt(z[:, c * TH + t * TT + n * 128: c * TH + t * TT + (n + 1) * 128], ps[b][:, n * 128:(n + 1) * 128],
                       bsbc[:, c * 128:(c + 1) * 128], ALU.add, [("ps", b), "bsbc"], [("z", c, t)])
        for c in range(NCH):
            s = wload(w_gin[16 + c])
            for t in range(2):
                b = bank()
                mm(ps[b][:, :], [(wk(s, kc), ha(kc, t)) for kc in range(16)], [("W", s)] + hkeys(t), [("ps", b)])
                tt(za(c, t), ps[b][:, :], za(c, t), ALU.mult, [("ps", b), ("z", c, t)], [("z", c, t)])
        out_proj(w_gout, 0, zsrc, zkeys, l, 2, tgen=tgen, t_outer=True)

    def mla_kside(half):
        sp_dma(ropeCk[0:64, :], rope_d[half, :, 0:TH], writes=RCK, chname="rp", nch=2)
        sp_dma(ropeSk[0:64, :], rope_d[half, :, TH:2 * TH], writes=RSK, chname="rp", nch=2)
        for cc in range(4):
            s = wload(w_kva[cc])
            for t in range(2):
                b = bank()
                mm(ps[b][:, :], [(wk(s, kc), ha(kc, t)) for kc in range(16)], [("W", s)] + hkeys(t), [("ps", b)])
                act(zf[:, cc * TH + t * TT: cc * TH + (t + 1) * TT], ps[b][:, :], AF.Copy, [("ps", b)], zfk(cc, t))
                q = st["sq"] % 2
                st["sq"] += 1
                act(sq[q][:, :], ps[b][:, :], AF.Square, [("ps", b)], [("sq", q)])
                P.op("pe", lambda e, q=q, cc=cc, t=t: e.matmul(ps[5 + t][:, :], lhsT=ones512[:, :], rhs=sq[q][:, :],
                                                             start=(cc == 0), stop=(cc == 3)),
                     [("sq", q), "ones512"], [("ps", 5 + t)])
        for t in range(2):
            act(rstd[:, t * TT:(t + 1) * TT], ps[5 + t][:, :], AF.Sqrt, [("ps", 5 + t)], [("rstd", t)], bias=EPS, scale=1.0)
            recip(rstd[:, t * TT:(t + 1) * TT], rstd[:, t * TT:(t + 1) * TT], [("rstd", t)], [("rstd", t)])
        for cc in range(4):
            for t in range(2):
                zz = zf[:, cc * TH + t * TT: cc * TH + (t + 1) * TT]
                stt(zz, zz, gkv[:, cc:cc + 1], rstd[:, t * TT:(t + 1) * TT], ALU.mult, ALU.mult,
                    zfk(cc, t) + ["gkv", ("rstd", t)], zfk(cc, t))
                tok = half * TH + t * TT
                sp_dma(ckvo[:, cc * 2048 + tok: cc * 2048 + tok + TT], zz, reads=zfk(cc, t), chname="ko", nch=2)
                act(ckv_all[:, cc * NKEY + 512 + tok: cc * NKEY + 512 + tok + TT], zz, AF.Copy,
                    zfk(cc, t), [("ckv_all", cc)])
        s = wload(w_kva[4])
        for t in range(2):
            b1 = bank()
            mm(ps[b1][0:64, :], [(wk(s, kc, 0, 64), ha(kc, t)) for kc in range(16)], [("W", s)] + hkeys(t), [("ps", b1)])
            b2 = bank()
            mm(ps[b2][0:64, :], [(wk(s, kc, 64, 128), ha(kc, t)) for kc in range(16)], [("W", s)] + hkeys(t), [("ps", b2)])
            r1 = st["xr"] % 2
            st["xr"] += 1
            r2 = st["xr"] % 2
            st["xr"] += 1
            tt(xr[r1][0:64, :], ps[b1][0:64, :], ropeCk[0:64, t * TT:(t + 1) * TT], ALU.mult, [("ps", b1)] + RCK, [("xr", r1)])
            tt(xr[r2][0:64, :], ps[b2][0:64, :], ropeSk[0:64, t * TT:(t + 1) * TT], ALU.mult, [("ps", b2)] + RSK, [("xr", r2)])
            tt(xr[r1][0:64, :], xr[r1][0:64, :], xr[r2][0:64, :], ALU.add, [("xr", r1), ("xr", r2)], [("xr", r1)])
            tok = half * TH + t * TT
            sp_dma(kpeo[:, tok:tok + TT], xr[r1][0:64, :], reads=[("xr", r1)], chname="ko", nch=2)
            act(kpe_all[0:64, 512 + tok: 512 + tok + TT], xr[r1][0:64, :], AF.Copy, [("xr", r1)], ["kpe_all"])

    def mla_attend(l, half, rope_loaded, tgen=None):
        P.op("pool", lambda e: e.dma_start(out=qpe[0][64:72, :], in_=qmask_d[half, :, 0:TT]), writes=[("qpe", 0)], ch=("w", "q"))
        P.op("pool", lambda e: e.dma_start(out=qpe[1][64:72, :], in_=qmask_d[half, :, TT:2 * TT]), writes=[("qpe", 1)], ch=("w", "q"))
        for cc in range(4):
            s = wload(w_qa[cc])
            for t in range(2):
                b = bank()
                mm(ps[b][:, :], [(wk(s, kc), ha(kc, t)) for kc in range(16)], [("W", s)] + hkeys(t), [("ps", b)])
                act(qan[:, cc * TH + t * TT: cc * TH + (t + 1) * TT], ps[b][:, :], AF.Copy, [("ps", b)], [("qan", cc, t)])
                q = st["sq"] % 2
                st["sq"] += 1
                act(sq[q][:, :], ps[b][:, :], AF.Square, [("ps", b)], [("sq", q)])
                P.op("pe", lambda e, q=q, cc=cc, t=t: e.matmul(ps[5 + t][:, :], lhsT=ones512[:, :], rhs=sq[q][:, :],
                                                             start=(cc == 0), stop=(cc == 3)),
                     [("sq", q), "ones512"], [("ps", 5 + t)])
        for t in range(2):
            act(rstd[:, t * TT:(t + 1) * TT], ps[5 + t][:, :], AF.Sqrt, [("ps", 5 + t)], [("rstd", t)], bias=EPS, scale=1.0)
            recip(rstd[:, t * TT:(t + 1) * TT], rstd[:, t * TT:(t + 1) * TT], [("rstd", t)], [("rstd", t)])
        for cc in range(4):
            for t in range(2):
                qq = qan[:, cc * TH + t * TT: cc * TH + (t + 1) * TT]
                stt(qq, qq, gq[:, cc:cc + 1], rstd[:, t * TT:(t + 1) * TT], ALU.mult, ALU.mult,
                    [("qan", cc, t), "gq", ("rstd", t)], [("qan", cc, t)])
        P.fence(HKEYS, TMPKEYS)
        if not rope_loaded:
            sp_dma(ropeC[0:64, :], rope_d[half, :, 0:TH], writes=["ropeC"], chname="rp", nch=2)
            sp_dma(ropeS[0:64, :], rope_d[half, :, TH:2 * TH], writes=["ropeS"], chname="rp", nch=2)
        qanK = [("qan", cc, t) for cc in range(4) for t in range(2)]
        ckvK = [("ckv_all", cc) for cc in range(4)]
        pcount = 0
        for hd in range(16):
            if hd % 2 == 0:
                s_qb = wload(w_qb[hd // 2])
                s_kvb = wload(w_kvb[hd // 2])
            ho = (hd % 2) * 256
            for k5 in range(5):
                b = 6 + (st["bank"] % 2)
                st["bank"] += 1
                mm(ps[b][:, :], [(wk(s_kvb, kc, ho, ho + 128, KC=4), ckv_all[:, kc * NKEY + k5 * 512: kc * NKEY + (k5 + 1) * 512])
                                 for kc in range(4)], [("W", s_kvb)] + ckvK, [("ps", b)])
                act(knope[:, k5 * 512:(k5 + 1) * 512], ps[b][:, :], AF.Copy, [("ps", b)], ["knope"])
            for k5 in range(5):
                b = 6 + (st["bank"] % 2)
                st["bank"] += 1
                for n in range(4):
                    kt = k5 * 4 + n
                    mm(ps[b][:, n * 128:(n + 1) * 128],
                       [(ckv_all[:, kc * NKEY + kt * 128: kc * NKEY + (kt + 1) * 128], wk(s_kvb, kc, ho + 128, ho + 256, KC=4))
                        for kc in range(4)], [("W", s_kvb)] + ckvK, [("ps", b)])
                P.op("dve", lambda e, b=b, k5=k5: e.tensor_copy(out=vh[:, k5 * 512:(k5 + 1) * 512], in_=ps[b][:, :]),
                     [("ps", b)], ["vh"])
            for qt in range(2):
                b = 6 + (st["bank"] % 2)
                st["bank"] += 1
                qq = st["xr"] % 2
                mm(ps[b][:, :], [(wk(s_qb, kc, ho, ho + 128, KC=4), qan[:, kc * TH + qt * TT: kc * TH + (qt + 1) * TT])
                                 for kc in range(4)], [("W", s_qb)] + qanK, [("ps", b)])
                qb_ = (hd * 2 + qt) % 2
                act(qn[qb_][:, :], ps[b][:, :], AF.Copy, [("ps", b)], [("qn%d" % qb_)])
                b1 = 6 + (st["bank"] % 2)
                st["bank"] += 1
                mm(ps[b1][0:64, :], [(wk(s_qb, kc, ho + 128, ho + 192, KC=4), qan[:, kc * TH + qt * TT: kc * TH + (qt + 1) * TT])
                                     for kc in range(4)], [("W", s_qb)] + qanK, [("ps", b1)])
                b2 = 6 + (st["bank"] % 2)
                st["bank"] += 1
                mm(ps[b2][0:64, :], [(wk(s_qb, kc, ho + 192, ho + 256, KC=4), qan[:, kc * TH + qt * TT: kc * TH + (qt + 1) * TT])
                                     for kc in range(4)], [("W", s_qb)] + qanK, [("ps", b2)])
                r1 = st["xr"] % 2
                st["xr"] += 1
                r2 = st["xr"] % 2
                st["xr"] += 1
                tt(xr[r1][0:64, :], ps[b1][0:64, :], ropeC[0:64, qt * TT:(qt + 1) * TT], ALU.mult, [("ps", b1), "ropeC"], [("xr", r1)])
                tt(xr[r2][0:64, :], ps[b2][0:64, :], ropeS[0:64, qt * TT:(qt + 1) * TT], ALU.mult, [("ps", b2), "ropeS"], [("xr", r2)])
                tt(qpe[qt][0:64, :], xr[r1][0:64, :], xr[r2][0:64, :], ALU.add, [("xr", r1), ("xr", r2)], [("qpe", qt)])
                SB = (0, 1, 2, 5)

                def score(kt):
                    sbk = SB[kt % 4]
                    mm(ps[sbk][:, :], [(knope[:, kt * 128:(kt + 1) * 128], qn[qb_][:, :]),
                                       (kpe_all[0:72, kt * 128:(kt + 1) * 128], qpe[qt][0:72, :])],
                       ["knope", "qn%d" % qb_, "kpe_all", ("qpe", qt)], [("ps", sbk)])
                score(0)
                score(1)
                for kt in range(20):
                    if kt + 2 < 20:
                        score(kt + 2)
                    sbk = SB[kt % 4]
                    pb = pcount % 3
                    pcount += 1
                    act(pT[pb][:, :], ps[sbk][:, :], AF.Exp, [("ps", sbk)], ["pT%d" % pb], scale=SCALE)
                    P.op("pe", lambda e, kt=kt, pb=pb: e.matmul(ps[3][:, :], lhsT=vh[:, kt * 128:(kt + 1) * 128], rhs=pT[pb][:, :],
                                                              start=(kt == 0), stop=(kt == 19)),
                         ["vh", "pT%d" % pb], [("ps", 3)])
                    P.op("pe", lambda e, kt=kt, pb=pb: e.matmul(ps[4][:, :], lhsT=ones1[:, :], rhs=pT[pb][:, :],
                                                              start=(kt == 0), stop=(kt == 19)),
                         ["ones1", "pT%d" % pb], [("ps", 4)])
                recip(rec[:, :], ps[4][:, :], [("ps", 4)], ["rec"])
                tt(za(hd, qt), ps[3][:, :], rec[:, :], ALU.mult, [("ps", 3), "rec"], [("z", hd, qt)])
        P.fence(TMPKEYS, HKEYS)
        out_proj(w_o, 0, zsrc, zkeys, l, 2, tgen=tgen, t_outer=True)

    def final_out(half):
        for t in range(2):
            sb_ = 5 + t
            for c in range(NCH):
                q = st["sq"] % 2
                st["sq"] += 1
                act(sq[q][:, :], xa(c, t), AF.Square, [xk(c, t)], [("sq", q)])
                P.op("pe", lambda e, q=q, c=c, sb_=sb_: e.matmul(ps[sb_][:, :], lhsT=ones2048[:, :], rhs=sq[q][:, :],
                                                              start=(c == 0), stop=(c == NCH - 1)),
                     [("sq", q), "ones2048"], [("ps", sb_)])
            act(rstd[:, t * TT:(t + 1) * TT], ps[sb_][:, :], AF.Sqrt, [("ps", sb_)], [("rstd", t)], bias=EPS, scale=1.0)
            recip(rstd[:, t * TT:(t + 1) * TT], rstd[:, t * TT:(t + 1) * TT], [("rstd", t)], [("rstd", t)])
        for c in range(NCH):
            for t in range(2):
                stt(xa(c, t), xa(c, t), fn[:, c:c + 1], rstd[:, t * TT:(t + 1) * TT], ALU.mult, ALU.mult,
                    [xk(c, t), "fn", ("rstd", t)], [xk(c, t)])
            store_x(half, yT, c)

    orders = [(0, 1), (1, 0), (0, 1), (1, 0)]
    skip_mlp_last = bool(dbg and dbg.get("skip_mlp_last"))
    for l in range(NL):
        kind, jl = l % 3, l // 3
        last_layer = (l == NL - 1)
        if not st.get("pre_t0"):
            layer_scales(l)
        if l == 1:
            P.fence(CONVK, KVK + GMLPK)
            for cc in range(4):
                P.op("pool", lambda e, cc=cc: e.dma_start(out=ckv_all[:, cc * NKEY: cc * NKEY + 512],
                                                         in_=cckv_d[:, cc * 512:(cc + 1) * 512]),
                     writes=[("ckv_all", cc)], ch=("w", "c"))
            P.op("pool", lambda e: e.dma_start(out=kpe_all[0:64, 0:512], in_=ckpe_d[:, :]), writes=["kpe_all"], ch=("w", "c"))
            P.op("pool", lambda e: e.dma_start(out=kpe_all[64:72, :], in_=kmask_d[:, :]), writes=["kpe_all"], ch=("w", "c"))
            sp_dma(bsbc[:, :], bsbc_d[:, :], writes=["bsbc"], chname="msk", nch=1)
            P.op("pool", lambda e: e.dma_start(out=wsT[:, :], in_=wsT_d[:, :]), writes=["wsT"], ch=("w", "i"))
        if l == 2:
            P.fence(GMLPK, MLAK)
            P.fence([("psb", i) for i in range(8)], [("ps", 7)])
        if l == 3:
            P.fence(KVK + MLAK, CONVK)
        for oi, half in enumerate(orders[l]):
            resident = (l > 0 and oi == 0)
            if not resident:
                load_x(half, xT if l == 0 else xs)
            do_mlp = not (last_layer and skip_mlp_last)
            will_store = (oi == 0) and not last_layer
            need_xcol = (l + 1 < NL) and ((l + 1) % 3 == 0)
            nxt_mla = (l + 1 < NL) and ((l + 1) % 3 == 2)
            G2 = None
            if do_mlp:
                layer_scales2(l)
                G2 = norm_gen(l, 2, 0)
            if kind == 2:
                if oi == 1:
                    norm_to_h(l, 1)
                mla_attend(l, half, rope_loaded=False, tgen=G2)
            else:
                if st.get("pre_t0") and oi == 0:
                    run_gen(norm_gen(l, 1, 1))
                else:
                    norm_to_h(l, 1)
                if kind == 0:
                    conv_mixer(l, jl, half, first=(oi == 0), tgen=G2)
                else:
                    gmlp_mixer(l, tgen=G2)
            st["pre_t0"] = False
            if do_mlp:
                run_gen(G2)
                run_gen(norm_gen(l, 2, 1))
                Gn = None
                if nxt_mla:
                    layer_scales(l + 1)
                    Gn = norm_gen(l + 1, 1, 0)
                elif oi == 1 and not last_layer and (l + 1) % 3 != 2:
                    layer_scales(l + 1)
                    Gn = norm_gen(l + 1, 1, 0)
                    st["pre_t0"] = True

                def after_chunk(j, half=half, ws=(will_store and not nxt_mla and not need_xcol)):
                    if ws:
                        store_x(half, xs, j)
                mlp(l, after=after_chunk, tgen=Gn)
                run_gen(Gn)
            if need_xcol:
                col = TH - 1 if half == 0 else 0
                dst = xcol[:, 0:16] if half == 0 else xcol[:, 16:32]
                P.op("dve", lambda e, col=col, dst=dst: e.tensor_copy(out=dst, in_=x[:, col:NCH * TH:TH]),
                     [xk(c, t) for c in range(NCH) for t in range(2)], ["xcol"])
            if nxt_mla:
                if not do_mlp:
                    layer_scales(l + 1)
                    run_gen(norm_gen(l + 1, 1, 0))
                run_gen(norm_gen(l + 1, 1, 1))
                mla_kside(half)
                if oi == 0:
                    layer_scales(l)
            if last_layer:
                final_out(half)
            elif will_store and (nxt_mla or need_xcol):
                for j in range(NCH):
                    store_x(half, xs, j)

    P.op("sp", lambda e: None, reads=[], writes=[], ch=None)
    fin = P.ops[-1]
    for i, o in enumerate(P.ops[:-1]):
        if o[3] is not None and o[3][0] in ("yst", "ko"):
            fin[2].add(i)
    P.lower(nc, es)
    es.close()
    return nc, P


def _blk(W, KC=16):
    K, N = W.shape
    assert K == KC * 128
    ncol = 2048 // KC
    nb = N // ncol
    return np.ascontiguousarray(W.reshape(KC, 128, nb, ncol).transpose(2, 1, 0, 3)).reshape(nb, 128, KC * ncol)


def _fm(v):
    v = np.asarray(v, np.float32)
    lead = int(np.prod(v.shape[:-1])) if v.ndim > 1 else 1
    C = v.shape[-1] // 128
    return np.ascontiguousarray(v.reshape(lead, C, 128).transpose(2, 0, 1)).reshape(128, lead * C)


def _swap_cols(W):
    idx = np.concatenate([np.arange(16, 32), np.arange(0, 16), np.arange(48, 64), np.arange(32, 48)])
    return W[..., idx]


def _rope_tables():
    L = 2048
    rows = np.repeat(np.arange(L // 64), 64).astype(np.float32)
    cols = np.tile(np.arange(64), L // 64).astype(np.float32)
    inv = (1.0 / (10000.0 ** (np.arange(16, dtype=np.float32) / 16))).astype(np.float32)
    ar = rows[:, None] * inv
    ac = cols[:, None] * inv
    C = np.concatenate([np.cos(ar), np.cos(ar), np.cos(ac), np.cos(ac)], 1)
    S = np.concatenate([-np.sin(ar), np.sin(ar), -np.sin(ac), np.sin(ac)], 1)
    return C.astype(np.float32), S.astype(np.float32)


def _prep_shared(inp):
    g = {k: np.asarray(v, np.float32) for k, v in inp.items()}
    sh = {}
    sh["adab"] = _fm(g["ada_b"])
    sh["n1"] = _fm(g["norm1"])
    sh["n2"] = _fm(g["norm2"])
    sh["fn"] = _fm(g["final_norm"])
    sh["convw"] = _fm(g["conv_w"])
    sh["gv"] = _fm(g["gmlp_g_v"][0])
    sh["bsbc"] = np.ascontiguousarray(np.broadcast_to(g["gmlp_b_s"][0].reshape(1, 2048), (128, 2048)))
    sh["wsT"] = np.ascontiguousarray(g["gmlp_w_s"][0].transpose(2, 0, 1)).reshape(128, 2048)
    sh["gq"] = _fm(g["mla_g_q"][0])
    sh["gkv"] = _fm(g["mla_g_kv"][0])
    sh["ident"] = np.eye(128, dtype=np.float32)
    sh["w_ada"] = _blk(g["ada_w"].transpose(1, 0, 2).reshape(2048, 4 * 12288)) if False else \
        np.concatenate([_blk(g["ada_w"][l]) for l in range(4)], 0)
    cin = []
    for jl in range(2):
        W = g["conv_w_in"][jl]
        b = _blk(W)
        order = [k * 16 + c for c in range(16) for k in range(3)]
        cin.append(b[order])
    sh["w_cin"] = np.concatenate(cin, 0)
    sh["w_cout"] = np.concatenate([_blk(g["conv_w_out"][jl]) for jl in range(2)], 0)
    gb = _blk(g["gmlp_w_in"][0])
    sh["w_gin"] = np.concatenate([gb[16:32], gb[0:16]], 0)
    sh["w_gout"] = _blk(g["gmlp_w_out"][0])
    sh["w_qa"] = _blk(g["mla_w_q_a"][0])
    kva = g["mla_w_kv_a"][0]
    kva_ext = np.concatenate([kva, _swap_cols(kva[:, 512:576])], 1)
    sh["w_kva"] = _blk(kva_ext)
    qb = g["mla_w_q_b"][0].reshape(512, 16, 192)
    qb_ext = np.concatenate([qb, _swap_cols(qb[:, :, 128:192])], 2).reshape(512, 16 * 256)
    sh["w_qb"] = _blk(qb_ext, KC=4)
    sh["w_kvb"] = _blk(g["mla_w_kv_b"][0], KC=4)
    sh["w_o"] = _blk(g["mla_w_o"][0])
    sh["w_1"] = np.concatenate([_blk(g["mlp_w1"][l]) for l in range(4)], 0)
    w2 = []
    for l in range(4):
        for gq in range(4):
            w2.append(_blk(g["mlp_w2"][l][gq * 2048:(gq + 1) * 2048]))
    sh["w_2"] = np.concatenate(w2, 0)
    return sh


def _prep_core(inp, r):
    g = inp
    pc = {}
    if r < 4:
        X = np.asarray(g["x_prompt"][8 * r: 8 * r + 8], np.float32).reshape(2048, D)
        cond = np.asarray(g["c_ctx"], np.float32)
    else:
        X = np.asarray(g["x_sample"][r - 4], np.float32)
        cond = np.asarray(g["c"][r - 4], np.float32)
    XT = X.reshape(2, TH, NCH, 128).transpose(0, 3, 2, 1)
    pc["xT"] = np.ascontiguousarray(XT).reshape(2, 128, NCH * TH)
    hal = np.stack([X[TH - 1], X[TH]], 0)
    pc["xhalo"] = _fm(hal)
    pc["cond"] = _fm(cond)
    tok = np.arange(2048)
    if r < 4:
        mL = (tok % 256 != 0).astype(np.float32)
        mR = (tok % 256 != 255).astype(np.float32)
    else:
        mL = (tok != 0).astype(np.float32)
        mR = (tok != 2047).astype(np.float32)
    m = np.stack([np.concatenate([mL[hf * TH:(hf + 1) * TH], mR[hf * TH:(hf + 1) * TH]]) for hf in range(2)], 0)
    pc["masks"] = np.ascontiguousarray(np.broadcast_to(m[:, None, :], (2, 128, 2 * TH)))
    if r < 4:
        C = np.ones((2048, 64), np.float32)
        S = np.zeros((2048, 64), np.float32)
    else:
        C, S = _rope_tables()
    pc["rope"] = np.ascontiguousarray(np.stack(
        [np.concatenate([C[hf * TH:(hf + 1) * TH].T, S[hf * TH:(hf + 1) * TH].T], 1) for hf in range(2)], 0))
    km = np.zeros((8, NKEY), np.float32)
    qm = np.zeros((2, 8, TH), np.float32)
    if r < 4:
        seq = tok // 256
        km[:, :] = NEG
        for s_ in range(8):
            km[s_, 512 + np.nonzero(seq == s_)[0]] = 0.0
        for hf in range(2):
            for t in range(TH):
                qm[hf, seq[hf * TH + t], t] = 1.0
        cck = np.zeros((128, 4 * 512), np.float32)
        ckp = np.zeros((64, 512), np.float32)
    else:
        qm[:, 0, :] = 1.0
        cc = np.asarray(g["cache_ckv"][r - 4, 0], np.float32)
        cck = np.ascontiguousarray(cc.reshape(512, 4, 128).transpose(2, 1, 0)).reshape(128, 4 * 512)
        ckp = np.ascontiguousarray(np.asarray(g["cache_kpe"][r - 4, 0], np.float32).T)
    pc["kmask"] = km
    pc["qmask"] = qm
    pc["cckv"] = cck
    pc["ckpe"] = ckp
    return pc


_CACHE = {}


def kernel(**inputs):
    sh = _prep_shared(inputs)
    in_maps = []
    for r in range(8):
        m = dict(sh)
        m.update(_prep_core(inputs, r))
        in_maps.append(m)
    if "nc" not in _CACHE:
        _CACHE["nc"] = build_program()[0]
    nc = _CACHE["nc"]
    res = run_bass_kernel_spmd(nc, in_maps, core_ids=list(range(8)))
    outs = res.results
    y = []
    for r in range(8):
        yT = np.asarray(outs[r]["yT"]).reshape(2, 128, NCH, TH)
        y.append(np.ascontiguousarray(yT.transpose(0, 3, 2, 1)).reshape(2048, D))
    y_prompt = np.stack(y[0:4], 0).reshape(32, 256, D).astype(np.float32)
    y_sample = np.stack(y[4:8], 0).reshape(4, 2048, D).astype(np.float32)
    ck = []
    kp = []
    for r in range(4):
        c = np.asarray(outs[r]["ckvo"]).reshape(128, 4, 2048)
        ck.append(np.ascontiguousarray(c.transpose(2, 1, 0)).reshape(2048, 512))
        kp.append(np.ascontiguousarray(np.asarray(outs[r]["kpeo"]).T))
    new_ckv = np.stack(ck, 0).reshape(32, 256, 512)[:, None].astype(np.float32)
    new_kpe = np.stack(kp, 0).reshape(32, 256, 64)[:, None].astype(np.float32)
    return (y_prompt, y_sample, np.ascontiguousarray(new_ckv), np.ascontiguousarray(new_kpe))
```
